# Optimizing a Trainium2 kernel written in Bass

```python
import math
import jax, jax.numpy as jnp
from jax import lax
import numpy as np

D_MODEL = 2048
BATCH = 2
SEQ = 4096
DEPTH = 1

HEAD_DIM = 128
DIL_PATTERNS = ((128, 1), (512, 4), (2048, 16))
A_HEADS_PER_GROUP = 4
A_HEADS = A_HEADS_PER_GROUP * len(DIL_PATTERNS)
A_OUT = A_HEADS_PER_GROUP * HEAD_DIM
B_HEADS = 8
B_WIDTH = B_HEADS * HEAD_DIM
MOBA_BLOCK = 256
MOBA_TOPK = 3
MOBA_CHUNK = 32
N_HEADS_TOTAL = A_HEADS + B_HEADS
N_BUCKETS = 32
MAX_DISTANCE = 2048
D_FF = 4 * D_MODEL
EPS = 1e-6
SCALE = HEAD_DIM ** -0.5
A_QKV = 3 * A_HEADS * HEAD_DIM
B_QKV = 3 * B_WIDTH
IN_WIDTH = A_QKV + B_QKV + 2 * D_MODEL

kernel_name = "hybrid_dilated_moba_gated_block"


def rms_norm(x, g):
    x32 = x.astype(jnp.float32)
    y = x32 * lax.rsqrt(jnp.mean(x32 * x32, axis=-1, keepdims=True) + EPS)
    return y.astype(x.dtype) * g


def rel_bucket(dist):
    max_exact = N_BUCKETS // 2
    d = jnp.maximum(dist, 0)
    df = jnp.maximum(d, 1).astype(jnp.float32)
    large = max_exact + (jnp.log(df / max_exact) / math.log(MAX_DISTANCE / max_exact)
                         * (N_BUCKETS - max_exact)).astype(jnp.int32)
    large = jnp.minimum(large, N_BUCKETS - 1)
    return jnp.where(d < max_exact, d, large)


def dilated_group(q, k, v, bias_tab, window, dilation):
    Bsz, H, S, hd = q.shape
    r = dilation
    W = window // r
    Q = W
    L = S // r
    nb = -(-L // Q)
    Lp = nb * Q

    def to_sub(t):
        t = t.reshape(Bsz, H, L, r, hd).transpose(0, 1, 3, 2, 4)
        t = jnp.pad(t, ((0, 0), (0, 0), (0, 0), (0, Lp - L), (0, 0)))
        return t.reshape(Bsz, H, r, nb, Q, hd)

    qs, ks, vs = to_sub(q), to_sub(k), to_sub(v)

    def with_prev(t):
        prev = jnp.pad(t, ((0, 0), (0, 0), (0, 0), (1, 0), (0, 0), (0, 0)))[:, :, :, :nb]
        return jnp.concatenate([prev, t], axis=4)

    kk, vv = with_prev(ks), with_prev(vs)
    a = jnp.arange(Q)[:, None]
    j = jnp.arange(2 * Q)[None, :]
    delta = a + Q - j
    in_band = (delta >= 0) & (delta <= W)
    valid_start = (jnp.arange(nb)[:, None, None] > 0) | (j[None] >= Q)
    mask = in_band[None] & valid_start
    bias = bias_tab[rel_bucket(delta * r)].transpose(2, 0, 1)

    logits = (jnp.einsum('bhrnqd,bhrnkd->bhrnqk', qs, kk).astype(jnp.float32) * SCALE
              + bias[:, None, None].astype(jnp.float32))
    logits = jnp.where(mask, logits, -jnp.inf)
    m = jnp.max(logits, axis=-1, keepdims=True)
    p = jnp.exp(logits - m)
    den = jnp.sum(p, axis=-1, keepdims=True)
    out = jnp.einsum('bhrnqk,bhrnkd->bhrnqd', (p / den).astype(v.dtype), vv)
    lse = (m + jnp.log(den))[..., 0]

    out = out.reshape(Bsz, H, r, Lp, hd)[:, :, :, :L].transpose(0, 1, 3, 2, 4).reshape(Bsz, H, S, hd)
    lse = lse.reshape(Bsz, H, r, Lp)[:, :, :, :L].transpose(0, 1, 3, 2).reshape(Bsz, H, S)
    return out, lse


def moba_attention(q, k, v, bias_tab):
    Bsz, H, S, hd = q.shape
    BS = MOBA_BLOCK
    nb = -(-S // BS)
    Sp = nb * BS
    pad = ((0, 0), (0, 0), (0, Sp - S), (0, 0))
    kb = jnp.pad(k, pad).reshape(Bsz, H, nb, BS, hd)
    vb = jnp.pad(v, pad).reshape(Bsz, H, nb, BS, hd)

    kmean = jnp.mean(kb.astype(jnp.float32), axis=3)
    gate = jnp.einsum('bhsd,bhnd->bhsn', q.astype(jnp.float32), kmean)
    pos = jnp.arange(S)
    past = jnp.arange(nb)[None, :] < (pos // BS)[:, None]
    gate = jnp.where(past, gate, -jnp.inf)
    n_sel = min(MOBA_TOPK, nb)
    top_s, top_i = lax.top_k(gate, n_sel)
    top_ok = top_s > -jnp.inf

    C = MOBA_CHUNK
    nc = S // C
    b_ix = jnp.arange(Bsz)[:, None, None, None]
    h_ix = jnp.arange(H)[None, :, None, None]
    bias_h = bias_tab.T

    def chunk(c):
        s0 = c * C
        qc = lax.dynamic_slice_in_dim(q, s0, C, axis=2)
        ic = lax.dynamic_slice_in_dim(top_i, s0, C, axis=2)
        okc = lax.dynamic_slice_in_dim(top_ok, s0, C, axis=2)
        qpos = s0 + jnp.arange(C)
        kg = kb[b_ix, h_ix, ic]
        vg = vb[b_ix, h_ix, ic]
        dist_sel = qpos[:, None, None] - (ic[..., None] * BS + jnp.arange(BS))
        l_sel = (jnp.einsum('bhcd,bhckjd->bhckj', qc, kg).astype(jnp.float32) * SCALE
                 + bias_h[h_ix[..., None], rel_bucket(dist_sel)].astype(jnp.float32))
        l_sel = jnp.where(okc[..., None], l_sel, -jnp.inf).reshape(Bsz, H, C, n_sel * BS)

        blk = s0 // BS
        ko = lax.dynamic_slice_in_dim(kb, blk, 1, axis=2)[:, :, 0]
        vo = lax.dynamic_slice_in_dim(vb, blk, 1, axis=2)[:, :, 0]
        dist_own = qpos[:, None] - (blk * BS + jnp.arange(BS))[None, :]
        l_own = (jnp.einsum('bhcd,bhjd->bhcj', qc, ko).astype(jnp.float32) * SCALE
                 + bias_h[:, rel_bucket(dist_own)][None].astype(jnp.float32))
        l_own = jnp.where(dist_own >= 0, l_own, -jnp.inf)

        p = jax.nn.softmax(jnp.concatenate([l_sel, l_own], axis=-1), axis=-1).astype(v.dtype)
        p_sel = p[..., :n_sel * BS].reshape(Bsz, H, C, n_sel, BS)
        p_own = p[..., n_sel * BS:]
        return (jnp.einsum('bhckj,bhckjd->bhcd', p_sel, vg)
                + jnp.einsum('bhcj,bhjd->bhcd', p_own, vo))

    out = lax.map(chunk, jnp.arange(nc))
    return out.transpose(1, 2, 0, 3, 4).reshape(Bsz, H, S, hd)


def head_rms(t, g):
    t32 = t.astype(jnp.float32)
    y = t32 * lax.rsqrt(jnp.mean(t32 * t32, axis=-1, keepdims=True) + EPS)
    return y.astype(t.dtype) * g


def setup_inputs(seed: int = 0) -> dict:
    key = jax.random.key(seed)
    ks = jax.random.split(key, 16)
    f32 = jnp.float32
    nrm = lambda k, shape, fan: jax.random.normal(k, shape, f32) * (fan ** -0.5)
    gain = lambda k, n: 1.0 + 0.02 * jax.random.normal(k, (n,), f32)
    return {
        "x": jax.random.normal(ks[0], (BATCH, SEQ, D_MODEL), f32),
        "g_mix": gain(ks[1], D_MODEL),
        "w_in": nrm(ks[2], (D_MODEL, IN_WIDTH), D_MODEL),
        "q_norm_a": gain(ks[3], HEAD_DIM),
        "k_norm_a": gain(ks[4], HEAD_DIM),
        "q_norm_b": gain(ks[5], HEAD_DIM),
        "k_norm_b": gain(ks[6], HEAD_DIM),
        "rel_bias": 0.5 * jax.random.normal(ks[7], (N_BUCKETS, N_HEADS_TOTAL), f32),
        "w_branch_a": nrm(ks[8], (A_OUT, D_MODEL), A_OUT),
        "w_branch_b": nrm(ks[9], (B_WIDTH, D_MODEL), B_WIDTH),
        "w_out": nrm(ks[10], (D_MODEL, D_MODEL), D_MODEL),
        "g_mlp": gain(ks[11], D_MODEL),
        "w_up": nrm(ks[12], (D_MODEL, D_FF), D_MODEL),
        "w_down": nrm(ks[13], (D_FF, D_MODEL), D_FF),
    }


def reference(x, g_mix, w_in, q_norm_a, k_norm_a, q_norm_b, k_norm_b, rel_bias,
              w_branch_a, w_branch_b, w_out, g_mlp, w_up, w_down):
    Bsz, S, D = x.shape
    for _ in range(DEPTH):
        h = rms_norm(x, g_mix)
        proj = h @ w_in
        a_qkv = proj[..., :A_QKV].reshape(Bsz, S, 3, A_HEADS, HEAD_DIM)
        b_qkv = proj[..., A_QKV:A_QKV + B_QKV].reshape(Bsz, S, 3, B_HEADS, HEAD_DIM)
        g_a = proj[..., A_QKV + B_QKV:A_QKV + B_QKV + D_MODEL]
        g_b = proj[..., A_QKV + B_QKV + D_MODEL:]

        qa = head_rms(a_qkv[:, :, 0], q_norm_a).transpose(0, 2, 1, 3)
        ka = head_rms(a_qkv[:, :, 1], k_norm_a).transpose(0, 2, 1, 3)
        va = a_qkv[:, :, 2].transpose(0, 2, 1, 3)
        outs, lses = [], []
        for gi, (win, dil) in enumerate(DIL_PATTERNS):
            hs = slice(gi * A_HEADS_PER_GROUP, (gi + 1) * A_HEADS_PER_GROUP)
            o, l = dilated_group(qa[:, hs], ka[:, hs], va[:, hs], rel_bias[:, hs], win, dil)
            outs.append(o)
            lses.append(l)
        wts = jax.nn.softmax(jnp.stack(lses, axis=0), axis=0)
        ya = jnp.sum(wts[..., None] * jnp.stack(outs, axis=0).astype(jnp.float32), axis=0).astype(x.dtype)
        ya = ya.transpose(0, 2, 1, 3).reshape(Bsz, S, A_OUT) @ w_branch_a

        qb = head_rms(b_qkv[:, :, 0], q_norm_b).transpose(0, 2, 1, 3)
        kb = head_rms(b_qkv[:, :, 1], k_norm_b).transpose(0, 2, 1, 3)
        vb = b_qkv[:, :, 2].transpose(0, 2, 1, 3)
        yb = moba_attention(qb, kb, vb, rel_bias[:, A_HEADS:])
        yb = yb.transpose(0, 2, 1, 3).reshape(Bsz, S, B_WIDTH) @ w_branch_b

        mixed = jax.nn.sigmoid(g_a) * ya + jax.nn.sigmoid(g_b) * yb
        x = x + mixed @ w_out

        h2 = rms_norm(x, g_mlp)
        x = x + jnp.square(jax.nn.relu(h2 @ w_up)) @ w_down
    return x
```

```python
import os
import numpy as np
import concourse.bass as bass
import concourse.mybir as mybir
from concourse.bass_utils import run_bass_kernel_spmd

F32 = mybir.dt.float32
BF16 = mybir.dt.bfloat16
U32 = mybir.dt.uint32
AF = mybir.ActivationFunctionType
ALU = mybir.AluOpType
AX = mybir.AxisListType

D_MODEL = 2048
SEQ = 4096
HD = 128
A_HEADS = 12
A_QKV = 3 * A_HEADS * HD
B_WIDTH = 1024
B_QKV = 3 * B_WIDTH
EPS = 1e-6
SCALE = HD ** -0.5
NEG = -30000.0
GROUPS = [[0, 1, 2, 3], [4, 5, 6, 7]]
ARENA_BYTES = 204 * 1024


class Ev:
    __slots__ = ("eng", "idx", "fn", "deps", "signal", "kind", "sem", "val", "sigcount", "inc")

    def __init__(self, eng, idx, fn):
        self.eng = eng
        self.idx = idx
        self.fn = fn
        self.deps = set()
        self.signal = False
        self.kind = "compute"
        self.sem = None
        self.val = 0
        self.sigcount = 0
        self.inc = 0


class Prog:
    ENGS = ("pe", "act", "dve", "pool", "sp")

    def __init__(self, nc):
        self.nc = nc
        self.ops = {e: [] for e in self.ENGS}
        self.last_w = {}
        self.readers = {}
        self.dma = {}
        self.psem = {e: nc.alloc_semaphore("prog_" + e) for e in self.ENGS}

    def _dep(self, ev, d):
        if d is ev or d is None:
            return
        ev.deps.add(d)
        if d.kind == "compute" and not (d.eng == "pe" and ev.eng == "pe"):
            d.signal = True

    def op(self, eng, fn, reads=(), writes=(), dma_key=None, inc=16, extra=()):
        ev = Ev(eng, len(self.ops[eng]), fn)
        for d in extra:
            self._dep(ev, d)
        for r in reads:
            self._dep(ev, self.last_w.get(r))
        for w in writes:
            self._dep(ev, self.last_w.get(w))
            for rd in self.readers.get(w, ()):
                self._dep(ev, rd)
        if dma_key is not None:
            st = self.dma.get(dma_key)
            if st is None:
                st = [self.nc.alloc_semaphore("dma_" + dma_key), 0, None]
                self.dma[dma_key] = st
            self._dep(ev, st[2])
            ev.kind = "dma"
            ev.sem = st[0]
            ev.inc = inc
            st[1] += inc
            ev.val = st[1]
            st[2] = ev
        for r in reads:
            self.readers.setdefault(r, []).append(ev)
        for w in writes:
            self.last_w[w] = ev
            self.readers[w] = []
        self.ops[eng].append(ev)
        return ev

    def barrier(self):
        lasts = []
        for e in self.ENGS:
            for ev in reversed(self.ops[e]):
                if ev.kind == "compute" and ev.fn is not None:
                    lasts.append(ev)
                    break
        dmas = [st[2] for st in self.dma.values() if st[2] is not None]
        for e in self.ENGS:
            self.op(e, None, extra=lasts + dmas)
        self.last_w = {}
        self.readers = {}

    def emit(self, block):
        for e in self.ENGS:
            c = 0
            for ev in self.ops[e]:
                if ev.kind == "compute" and ev.signal:
                    c += 1
                    ev.sigcount = c
                else:
                    ev.sigcount = -1
        for e in self.ENGS:
            for i, ev in enumerate(self.ops[e]):
                if ev.kind == "compute" and ev.signal and ev.fn is None:
                    raise RuntimeError("wait-only op cannot signal")

        def run(ename, eng):
            waited = {}
            for ev in self.ops[ename]:
                needs = {}
                for d in ev.deps:
                    if d.kind == "dma":
                        sem, val = d.sem, d.val
                    else:
                        if d.eng == "pe" and ename == "pe":
                            continue
                        sem, val = self.psem[d.eng], d.sigcount
                        assert val > 0
                    k = id(sem)
                    if k not in needs or needs[k][1] < val:
                        needs[k] = (sem, val)
                for k, (sem, val) in needs.items():
                    if waited.get(k, 0) < val:
                        eng.wait_ge(sem, val)
                        waited[k] = val
                if ev.fn is None:
                    continue
                ins = ev.fn(eng)
                if ev.kind == "dma":
                    ins.then_inc(ev.sem, ev.inc)
                elif ev.signal:
                    ins.then_inc(self.psem[ename], 1)

        block.tensor(lambda eng: run("pe", eng))
        block.scalar(lambda eng: run("act", eng))
        block.vector(lambda eng: run("dve", eng))
        block.gpsimd(lambda eng: run("pool", eng))
        block.sync(lambda eng: run("sp", eng))


class Arena:
    def __init__(self, nc, nbytes):
        self.t = nc.alloc_sbuf_tensor("arena", [128, nbytes // 2], BF16)
        self.nbytes = nbytes
        self.top = 0

    def mark(self):
        return self.top

    def reset(self, m):
        self.top = m

    def alloc(self, dtype, *dims):
        esz = 4 if dtype in (F32, U32) else 2
        n = 1
        for d in dims:
            n *= d
        nb = (n * esz + 63) // 64 * 64
        off = self.top
        self.top += nb
        assert self.top <= self.nbytes, f"arena overflow {self.top} > {self.nbytes}"
        v = self.t[:, off // 2: off // 2 + (n * esz) // 2]
        if esz == 4:
            v = v.bitcast(dtype)
        if len(dims) == 2:
            v = v.rearrange("p (a b) -> p a b", a=dims[0])
        elif len(dims) == 3:
            v = v.rearrange("p (a b c) -> p a b c", a=dims[0], b=dims[1])
        return v


def I(name, *args, **kw):
    return lambda e: getattr(e, name)(*args, **kw)


def build_program(debug=False):
    nc = bass.Bass("TRN2", target_bir_lowering=False)
    din = lambda name, shape, dt=F32: nc.dram_tensor(name, shape, dt, kind="ExternalInput")
    xfull = din("xfull", [SEQ, D_MODEL])
    xown = din("xown", [1024, D_MODEL])
    gmixb = din("gmixb", [128, D_MODEL])
    gmlpb = din("gmlpb", [128, D_MODEL])
    wA = din("wA", [128, 16, 1152])
    wB = din("wB", [128, 16, 768])
    qkg = din("qkg", [128, 4])
    biasA = din("biasA", [128, 4992])
    ttB = din("ttB", [128, 2 * 4096])
    identd = din("identd", [128, 128])
    ematd = din("ematd", [16, 16 * 128])
    wga = din("wga", [16, 128, 16, 128])
    wgb = din("wgb", [16, 128, 16, 128])
    wba = din("wba", [16, 128, 4, 128])
    wbb = din("wbb", [16, 128, 8, 128])
    wo = din("wo", [4, 128, 16, 512])
    wu = din("wu", [16, 128, 16, 512])
    wd = din("wd", [16, 128, 4, 2048])
    idxd = din("idxd", [128, 12], U32)
    out = nc.dram_tensor("out", [1024, D_MODEL], F32, kind="ExternalOutput")
    bounce = nc.dram_tensor("bounce", [4, 384, 1024], BF16)
    gath = nc.dram_tensor("gath", [4 * 1536, 1024], BF16)
    if debug:
        dbg_at = nc.dram_tensor("dbg_at", [1536, 1024], BF16, kind="ExternalOutput")
        dbg_x1 = nc.dram_tensor("dbg_x1", [1024, D_MODEL], F32, kind="ExternalOutput")

    P = Prog(nc)
    AR = Arena(nc, ARENA_BYTES)
    banks = [nc.alloc_psum_tensor(f"bank{i}", [128, 512], F32) for i in range(8)]
    bk32 = [b[:, :] for b in banks]
    bk16 = [b[:, :].bitcast(BF16) for b in banks]
    BK = [f"bank{i}" for i in range(8)]

    ident = AR.alloc(BF16, 128)
    ones = AR.alloc(BF16, 128)
    zeros = AR.alloc(BF16, 512)
    gb = AR.alloc(F32, D_MODEL)
    qk4 = AR.alloc(F32, 4)
    gsc = AR.alloc(F32, 2)
    small = AR.alloc(F32, 64)
    em = AR.alloc(BF16, 16, 128)
    idx = AR.alloc(U32, 12)
    base_mark = AR.mark()

    def ACT(func, out_, in_, scale=1.0, bias=0.0, accum=None):
        if accum is None:
            return I("activation", out=out_, in_=in_, func=func, bias=bias, scale=scale)
        return I("activation", out=out_, in_=in_, func=func, bias=bias, scale=scale, accum_out=accum)

    P.op("pool", I("dma_start", out=ident, in_=identd.ap()), writes=["ident"], dma_key="c0")
    P.op("pool", I("dma_start", out=em[0:16], in_=ematd.ap().rearrange("p (a b) -> p a b", a=16)), writes=["em"], dma_key="c1")
    P.op("sp", I("dma_start", out=gb, in_=gmixb.ap()), writes=["gb"], dma_key="c2")
    P.op("sp", I("dma_start", out=qk4, in_=qkg.ap()), writes=["qk4"], dma_key="c3")
    P.op("sp", I("dma_start", out=idx, in_=idxd.ap()), writes=["idx"], dma_key="c4")
    P.op("dve", I("memset", ones, 1.0), writes=["ones"])
    P.op("dve", I("memset", zeros, 0.0), writes=["zeros"])
    P.op("dve", I("tensor_scalar", out=gsc[:, 0:1], in0=qk4[:, 0:1], scalar1=qk4[:, 1:2], scalar2=SCALE,
                  op0=ALU.mult, op1=ALU.mult), reads=["qk4"], writes=["gsc0"])
    P.op("dve", I("tensor_scalar", out=gsc[:, 1:2], in0=qk4[:, 2:3], scalar1=qk4[:, 3:4], scalar2=SCALE,
                  op0=ALU.mult, op1=ALU.mult), reads=["qk4"], writes=["gsc1"])

    fe = {"cnt": 0}

    def fe_alloc(xs=True):
        fe["xs"] = [AR.alloc(F32, D_MODEL) for _ in range(2)] if xs else None
        fe["hb"] = [AR.alloc(BF16, D_MODEL) for _ in range(2)]

    def frontend_tile(src_ap, dst_fn, dst_names, tb, xs_res=None):
        t = fe["cnt"]
        fe["cnt"] += 1
        s = t % 2
        hb = fe["hb"][s]
        ss = small[:, 2 * s: 2 * s + 1]
        lt = small[:, 4 + 2 * s: 5 + 2 * s]
        rs = small[:, 8 + 2 * s: 9 + 2 * s]
        if xs_res is None:
            xs = fe["xs"][s]
            xr = [f"xs{s}"]
            P.op("sp", I("dma_start", out=xs, in_=src_ap), writes=xr, dma_key=f"xs{s}")
        else:
            xs = src_ap
            xr = list(xs_res)
        P.op("act", ACT(AF.Square, hb, xs, accum=ss), reads=xr, writes=[f"hb{s}", f"ss{s}"])
        P.op("act", ACT(AF.Ln, lt, ss, scale=1.0 / D_MODEL, bias=EPS), reads=[f"ss{s}"], writes=[f"lt{s}"])
        P.op("act", ACT(AF.Exp, rs, lt, scale=-0.5), reads=[f"lt{s}"], writes=[f"rs{s}"])
        P.op("dve", I("scalar_tensor_tensor", out=hb, in0=xs, scalar=rs, in1=gb, op0=ALU.mult, op1=ALU.mult),
             reads=xr + [f"rs{s}", "gb"], writes=[f"hb{s}"])
        for half in range(2):
            bi = tb[half]
            for c8 in range(8):
                c = half * 8 + c8
                P.op("pe", I("transpose", out=bk16[bi][:, c8 * 128:(c8 + 1) * 128], in_=hb[:, c * 128:(c + 1) * 128],
                             identity=ident), reads=[f"hb{s}", "ident"], writes=[BK[bi]])
            src = bk16[bi].rearrange("p (a b) -> p a b", a=8)
            dst = dst_fn(half)
            if half == 0:
                P.op("act", I("activation", out=dst, in_=src, func=AF.Copy), reads=[BK[bi]], writes=[dst_names[half]])
            else:
                P.op("dve", I("tensor_copy", out=dst, in_=src), reads=[BK[bi]], writes=[dst_names[half]])

    def project(W, cb, hT, hT_names, bi):
        for c in range(16):
            P.op("pe", I("matmul", bk32[bi], lhsT=W[:, c, cb * 128:(cb + 1) * 128], rhs=hT[:, c, :],
                         start=(c == 0), stop=(c == 15)), reads=hT_names + ["W"], writes=[BK[bi]])

    pp = {"cnt": 0}

    def pp_alloc():
        pp["sq"] = [AR.alloc(BF16, 512) for _ in range(2)]
        pp["lt"] = [AR.alloc(F32, 512) for _ in range(2)]
        pp["rs"] = [AR.alloc(F32, 512) for _ in range(2)]

    def qk_post(bi, sbank, dst, dst_name, gcol, perm):
        k = pp["cnt"] % 2
        pp["cnt"] += 1
        sq, lt, rs = pp["sq"][k], pp["lt"][k], pp["rs"][k]
        src = bk32[bi]
        P.op("act", ACT(AF.Square, sq, src), reads=[BK[bi]], writes=[f"sq{k}"])
        P.op("pe", I("matmul", bk32[sbank], lhsT=ones, rhs=sq, start=True, stop=True),
             reads=[f"sq{k}", "ones"], writes=[BK[sbank]])
        P.op("act", ACT(AF.Ln, lt, bk32[sbank], scale=1.0 / HD, bias=EPS), reads=[BK[sbank]], writes=[f"plt{k}"])
        P.op("act", ACT(AF.Exp, rs, lt, scale=-0.5), reads=[f"plt{k}"], writes=[f"prs{k}"])
        if perm:
            a0 = src.rearrange("p (a r) -> p r a", r=4)
            a1 = rs.rearrange("p (a r) -> p r a", r=4)
            o = dst.rearrange("p (r a) -> p r a", r=4)
        else:
            a0, a1, o = src, rs, dst
        if gcol is None:
            P.op("dve", I("tensor_tensor", out=o, in0=a0, in1=a1, op=ALU.mult),
                 reads=[BK[bi], f"prs{k}"], writes=[dst_name])
        else:
            P.op("dve", I("scalar_tensor_tensor", out=o, in0=a0, scalar=gsc[:, gcol:gcol + 1], in1=a1,
                          op0=ALU.mult, op1=ALU.mult), reads=[BK[bi], f"prs{k}", f"gsc{gcol}"], writes=[dst_name])

    def finalize_tile(obank, ocol, ybuf, yname, tbank, tcol, k):
        rec = small[:, 16 + k: 17 + k]
        P.op("dve", I("reciprocal", out=rec, in_=bk32[obank][:, ocol + 128: ocol + 129]),
             reads=[BK[obank]], writes=[f"rec{k}"])
        P.op("act", I("activation", out=ybuf, in_=bk32[obank][:, ocol: ocol + 128], func=AF.Copy, scale=rec),
             reads=[BK[obank], f"rec{k}"], writes=[yname])
        P.op("pe", I("transpose", out=bk16[tbank][:, tcol: tcol + 128], in_=ybuf, identity=ident),
             reads=[yname, "ident"], writes=[BK[tbank]])

    def zero_init(ob):
        P.op("pe", I("matmul", bk32[ob][:, 0:258], lhsT=zeros[:, 0:128], rhs=zeros[:, 0:258],
                     start=True, stop=False, skip_group_check=True), reads=["zeros"], writes=[BK[ob]])

    T0, T1, P0, P1, A0, A1, O0, O1 = range(8)
    xf = xfull.ap()
    cnt = {"p": 0, "a": 0, "pt": 0, "y": 0, "g": 0}

    def nxt(k, mod):
        v = cnt[k] % mod
        cnt[k] += 1
        return v

    def v_store(bi, VT, vs, h, sb, perm):
        if perm:
            P.op("act", I("activation", out=VT.rearrange("p (r a) -> p r a", r=4),
                          in_=bk32[bi].rearrange("p (a r) -> p r a", r=4), func=AF.Copy),
                 reads=[BK[bi]], writes=[f"VT{vs}"])
        else:
            P.op("act", I("activation", out=VT, in_=bk32[bi], func=AF.Copy), reads=[BK[bi]], writes=[f"VT{vs}"])
        for r in range(4):
            P.op("pe", I("transpose", out=bk16[T0][:, r * 128:(r + 1) * 128], in_=VT[:, r * 128:(r + 1) * 128],
                         identity=ident), reads=[f"VT{vs}", "ident"], writes=[BK[T0]])
        P.op("dve", I("tensor_copy", out=Vst[:, h * 32 + sb * 4: h * 32 + sb * 4 + 4, 0:128],
                      in_=bk16[T0][:, 0:512].rearrange("p (r a) -> p r a", r=4)),
             reads=[BK[T0]], writes=[f"V{h}_{sb}"])

    def cast_load(dst, src, name, key, chunk):
        P.op("pool", I("dma_start", out=dst.rearrange("p (a b) -> p a b", b=chunk),
                       in_=src.rearrange("p (a b) -> p a b", b=chunk)), writes=[name], dma_key=key)

    AR.reset(base_mark)
    fe_alloc()
    pp_alloc()
    W = AR.alloc(BF16, 16, 1152)
    KT = AR.alloc(BF16, 3, SEQ)
    Vst = AR.alloc(BF16, 3 * 32, 130)
    bA = AR.alloc(BF16, 4992)
    hTs = [AR.alloc(BF16, 16, 512) for _ in range(2)]
    QTs = [AR.alloc(BF16, 3, 512) for _ in range(2)]
    VTs = [AR.alloc(BF16, 512) for _ in range(2)]
    PTs = [AR.alloc(BF16, 512) for _ in range(3)]
    ybufs = [AR.alloc(BF16, 128) for _ in range(4)]
    ystage = [AR.alloc(BF16, 512) for _ in range(2)]

    for c4 in range(4):
        P.op("pool", I("dma_start", out=W[:, 4 * c4:4 * c4 + 4, :], in_=wA.ap()[:, 4 * c4:4 * c4 + 4, :]),
             writes=["W"], dma_key=f"w{c4}")
    cast_load(bA, biasA.ap(), "bA", "c5", 128)
    P.op("dve", I("memset", Vst[:, :, 128:130], 1.0), writes=["vones"])

    for sb in range(8):
        hs = sb % 2
        hT = hTs[hs]
        hT_names = [f"hT{hs}_{i}_{h}" for i in range(4) for h in range(2)]
        for i in range(4):
            r0 = sb * 512 + i * 128
            frontend_tile(xf[r0:r0 + 128, :],
                          (lambda i=i, hT=hT: lambda half: hT[:, half * 8:(half + 1) * 8, i * 128:(i + 1) * 128])(),
                          [f"hT{hs}_{i}_0", f"hT{hs}_{i}_1"], (T0, T1))
        QT = QTs[hs]
        for h in range(3):
            k = nxt("p", 2)
            project(W, 3 * h + 0, hT, hT_names, (P0, P1)[k])
            qk_post((P0, P1)[k], (A0, A1)[k], QT[:, h, :], f"QT{hs}_{h}", None, True)
            k = nxt("p", 2)
            project(W, 3 * h + 1, hT, hT_names, (P0, P1)[k])
            qk_post((P0, P1)[k], (A0, A1)[k], KT[:, h, sb * 512:(sb + 1) * 512], f"KT{h}_{sb}", 0, True)
            k = nxt("p", 2)
            project(W, 3 * h + 2, hT, hT_names, (P0, P1)[k])
            v_store((P0, P1)[k], VTs[k], k, h, sb, True)

        zero_init(O0)
        zero_init(O1)
        Ds = [1, 0] if sb >= 1 else [0]

        def pv(PT, pname, col0, h, kt, rr):
            ob = (O0, O1)[rr // 2]
            oc = (rr % 2) * 129
            P.op("pe", I("matmul", bk32[ob][:, oc:oc + 129], lhsT=PT[:, col0:col0 + 128],
                         rhs=Vst[:, h * 32 + kt, 0:129], start=False, stop=False, skip_group_check=True),
                 reads=[pname, f"V{h}_{kt // 4}", "vones"], writes=[BK[ob]])

        for D in Ds:
            for rp in range(4):
                kt = (sb - D) * 4 + rp
                ab = (A0, A1)[nxt("a", 2)]
                ps = nxt("pt", 3)
                PT = PTs[ps]
                P.op("pe", I("matmul", bk32[ab], lhsT=KT[:, 0, kt * 128:(kt + 1) * 128], rhs=QT[:, 0, :],
                             start=True, stop=False), reads=[f"KT0_{kt // 4}", f"QT{hs}_0"], writes=[BK[ab]])
                bo = (D * 4 + rp) * 512
                P.op("pe", I("matmul", bk32[ab], lhsT=ident, rhs=bA[:, bo:bo + 512], start=False, stop=True),
                     reads=["ident", "bA"], writes=[BK[ab]])
                P.op("act", ACT(AF.Exp, PT, bk32[ab]), reads=[BK[ab]], writes=[f"PT{ps}"])
                for rr in range(4):
                    pv(PT, f"PT{ps}", rr * 128, 0, kt, rr)
        packs = [(1, 4096 + (0 if D == 0 else 128), D) for D in Ds]
        packs += [(2, 4096 + 256 + 128 * m, m) for m in range(5) if sb - m >= 0]
        for (h, bo, dsb) in packs:
            ab = (A0, A1)[nxt("a", 2)]
            ps = nxt("pt", 3)
            PT = PTs[ps]
            for rr in range(4):
                kt = (sb - dsb) * 4 + rr
                P.op("pe", I("matmul", bk32[ab][:, rr * 128:(rr + 1) * 128], lhsT=KT[:, h, kt * 128:(kt + 1) * 128],
                             rhs=QT[:, h, rr * 128:(rr + 1) * 128], start=True, stop=False),
                     reads=[f"KT{h}_{kt // 4}", f"QT{hs}_{h}"], writes=[BK[ab]])
                P.op("pe", I("matmul", bk32[ab][:, rr * 128:(rr + 1) * 128], lhsT=ident, rhs=bA[:, bo:bo + 128],
                             start=False, stop=True), reads=["ident", "bA"], writes=[BK[ab]])
            P.op("act", ACT(AF.Exp, PT, bk32[ab]), reads=[BK[ab]], writes=[f"PT{ps}"])
            for rr in range(4):
                kt = (sb - dsb) * 4 + rr
                pv(PT, f"PT{ps}", rr * 128, h, kt, rr)
        for rr in range(4):
            finalize_tile((O0, O1)[rr // 2], (rr % 2) * 129, ybufs[rr], f"yb{rr}", T1, rr * 128, rr)
        ys = nxt("y", 2)
        yst = ystage[ys]
        P.op("dve", I("tensor_copy", out=yst.rearrange("p (a r) -> p r a", r=4),
                      in_=bk16[T1][:, 0:512].rearrange("p (r a) -> p r a", r=4)),
             reads=[BK[T1]], writes=[f"yst{ys}"])
        cch = sb // 2
        P.op("sp", I("dma_start", out=bounce.ap()[cch, 0:128, (sb % 2) * 512:(sb % 2) * 512 + 512], in_=yst),
             reads=[f"yst{ys}"], writes=[f"bncA{cch}_{sb % 2}"], dma_key=f"yst{ys}")

    P.barrier()

    AR.reset(base_mark)
    fe_alloc()
    pp_alloc()
    W = AR.alloc(BF16, 16, 768)
    KT = AR.alloc(BF16, 2, SEQ)
    Vst = AR.alloc(BF16, 2 * 32, 130)
    TT = AR.alloc(BF16, 2, 4096)
    hTs = [AR.alloc(BF16, 16, 512) for _ in range(2)]
    QTs = [AR.alloc(BF16, 2, 512) for _ in range(2)]
    VTs = [AR.alloc(BF16, 512) for _ in range(2)]
    PTs = [AR.alloc(BF16, 512) for _ in range(3)]
    ybufs = [AR.alloc(BF16, 128) for _ in range(4)]
    ystage = [AR.alloc(BF16, 512) for _ in range(2)]
    KMf = AR.alloc(F32, 2, 16)
    KMs = AR.alloc(F32, 2, 16)
    KMh = AR.alloc(BF16, 2, 16)
    KMl = AR.alloc(BF16, 2, 16)
    g16 = [AR.alloc(F32, 16) for _ in range(2)]
    m8 = [AR.alloc(F32, 8) for _ in range(2)]
    thr = [AR.alloc(F32, 1) for _ in range(2)]
    negm = [AR.alloc(BF16, 16) for _ in range(2)]
    negT = AR.alloc(BF16, 2, 512)

    for c4 in range(4):
        P.op("pool", I("dma_start", out=W[:, 4 * c4:4 * c4 + 4, :], in_=wB.ap()[:, 4 * c4:4 * c4 + 4, :]),
             writes=["W"], dma_key=f"w{c4}")
    cast_load(TT.rearrange("p a b -> p (a b)"), ttB.ap(), "TT", "c5", 1024)
    P.op("dve", I("memset", Vst[:, :, 128:130], 1.0), writes=["vones"])
    for k in range(2):
        P.op("dve", I("memset", g16[k], -1e30), writes=[f"g16_{k}"])
        P.op("dve", I("memset", negm[k], 0.0), writes=[f"negm{k}"])
    P.op("dve", I("memset", KMh, 0.0), writes=["KMh"])
    P.op("dve", I("memset", KMl, 0.0), writes=["KMl"])

    cc_evs = []
    for sb in range(8):
        hs = sb % 2
        hT = hTs[hs]
        hT_names = [f"hT{hs}_{i}_{h}" for i in range(4) for h in range(2)]
        for i in range(4):
            r0 = sb * 512 + i * 128
            frontend_tile(xf[r0:r0 + 128, :],
                          (lambda i=i, hT=hT: lambda half: hT[:, half * 8:(half + 1) * 8, i * 128:(i + 1) * 128])(),
                          [f"hT{hs}_{i}_0", f"hT{hs}_{i}_1"], (T0, T1))
        QT = QTs[hs]
        for h in range(2):
            k = nxt("p", 2)
            project(W, 3 * h + 0, hT, hT_names, (P0, P1)[k])
            qk_post((P0, P1)[k], (A0, A1)[k], QT[:, h, :], f"QT{hs}_{h}", None, False)
            k = nxt("p", 2)
            project(W, 3 * h + 1, hT, hT_names, (P0, P1)[k])
            qk_post((P0, P1)[k], (A0, A1)[k], KT[:, h, sb * 512:(sb + 1) * 512], f"KT{h}_{sb}", 1, False)
            k = nxt("p", 2)
            project(W, 3 * h + 2, hT, hT_names, (P0, P1)[k])
            v_store((P0, P1)[k], VTs[k], k, h, sb, False)
            sl = slice(2 * sb, 2 * sb + 2)
            P.op("dve", I("tensor_reduce", out=KMf[:, h, sl],
                          in_=KT[:, h, sb * 512:(sb + 1) * 512].rearrange("p (n t) -> p n t", n=2), axis=AX.X, op=ALU.add),
                 reads=[f"KT{h}_{sb}"], writes=[f"KMf{h}"])
            P.op("dve", I("tensor_scalar", out=KMs[:, h, sl], in0=KMf[:, h, sl], scalar1=1.0 / 256, scalar2=None,
                          op0=ALU.mult), reads=[f"KMf{h}"], writes=[f"KMs{h}"])
            P.op("dve", I("tensor_copy", out=KMh[:, h, sl], in_=KMs[:, h, sl]), reads=[f"KMs{h}"], writes=["KMh"])
            P.op("dve", I("tensor_tensor", out=KMl[:, h, sl], in0=KMs[:, h, sl], in1=KMh[:, h, sl], op=ALU.subtract),
                 reads=[f"KMs{h}", "KMh"], writes=["KMl"])

        for h in range(2):
            for i in range(4):
                nbq = (4 * sb + i) // 2
                k = nxt("g", 2)
                if nbq >= 1:
                    gcols = bk32[T1][:, 512 - 16 * (i + 1): 512 - 16 * i]
                    P.op("pe", I("matmul", gcols, lhsT=QT[:, h, i * 128:(i + 1) * 128], rhs=KMh[:, h, :],
                                 start=True, stop=False), reads=[f"QT{hs}_{h}", "KMh"], writes=[BK[T1]])
                    P.op("pe", I("matmul", gcols, lhsT=QT[:, h, i * 128:(i + 1) * 128], rhs=KMl[:, h, :],
                                 start=False, stop=True), reads=[f"QT{hs}_{h}", "KMl"], writes=[BK[T1]])
                    P.op("dve", I("tensor_copy", out=g16[k][:, 0:nbq], in_=gcols[:, 0:nbq]),
                         reads=[BK[T1]], writes=[f"g16_{k}"])
                    P.op("dve", I("max", out=m8[k], in_=g16[k]), reads=[f"g16_{k}"], writes=[f"m8_{k}"])
                    P.op("dve", I("tensor_scalar", out=thr[k], in0=m8[k][:, 2:3], scalar1=-1e29, scalar2=None,
                                  op0=ALU.max), reads=[f"m8_{k}"], writes=[f"thr{k}"])
                    P.op("dve", I("tensor_scalar", out=negm[k][:, 0:nbq], in0=g16[k][:, 0:nbq], scalar1=thr[k],
                                  scalar2=NEG, op0=ALU.is_lt, op1=ALU.mult),
                         reads=[f"g16_{k}", f"thr{k}"], writes=[f"negm{k}"])
                P.op("pe", I("transpose", out=bk16[T0][0:16, i * 128:(i + 1) * 128], in_=negm[k], identity=ident),
                     reads=[f"negm{k}", "ident"], writes=[BK[T0]])
            P.op("act", I("activation", out=negT[0:16, h, :], in_=bk16[T0][0:16, 0:512], func=AF.Copy),
                 reads=[BK[T0]], writes=[f"negT{h}"])

            zero_init(O0)
            zero_init(O1)
            for j in range(4 * sb + 4):
                u = max(0, j - 4 * sb)
                q0 = 128 * u
                N = 512 - q0
                n = j // 2
                need_mask = n <= 2 * sb
                ab = (A0, A1)[nxt("a", 2)]
                ps = nxt("pt", 3)
                PT = PTs[ps]
                P.op("pe", I("matmul", bk32[ab][:, 0:N], lhsT=KT[:, h, j * 128:(j + 1) * 128], rhs=QT[:, h, q0:512],
                             start=True, stop=False), reads=[f"KT{h}_{j // 4}", f"QT{hs}_{h}"], writes=[BK[ab]])
                t0 = 512 * sb + q0 - 128 * j
                P.op("pe", I("matmul", bk32[ab][:, 0:N], lhsT=ident, rhs=TT[:, h, t0:t0 + N], start=False,
                             stop=(not need_mask)), reads=["ident", "TT"], writes=[BK[ab]])
                if need_mask:
                    P.op("pe", I("matmul", bk32[ab][:, 0:N], lhsT=em[0:16, n, :], rhs=negT[0:16, h, q0:512],
                                 start=False, stop=True), reads=["em", f"negT{h}"], writes=[BK[ab]])
                P.op("act", I("activation", out=PT[:, 0:N], in_=bk32[ab][:, 0:N], func=AF.Exp),
                     reads=[BK[ab]], writes=[f"PT{ps}"])
                for i in range(u, 4):
                    ob = (O0, O1)[i // 2]
                    oc = (i % 2) * 129
                    c0 = i * 128 - q0
                    P.op("pe", I("matmul", bk32[ob][:, oc:oc + 129], lhsT=PT[:, c0:c0 + 128],
                                 rhs=Vst[:, h * 32 + j, 0:129], start=False, stop=False, skip_group_check=True),
                         reads=[f"PT{ps}", f"V{h}_{j // 4}", "vones"], writes=[BK[ob]])
            for i in range(4):
                finalize_tile((O0, O1)[i // 2], (i % 2) * 129, ybufs[i], f"yb{i}", T1, i * 128, i)
            ys = nxt("y", 2)
            yst = ystage[ys]
            P.op("dve", I("tensor_copy", out=yst, in_=bk16[T1][:, 0:512]), reads=[BK[T1]], writes=[f"yst{ys}"])
            cch = sb // 2
            P.op("sp", I("dma_start", out=bounce.ap()[cch, 128 * (1 + h):128 * (2 + h), (sb % 2) * 512:(sb % 2) * 512 + 512],
                         in_=yst), reads=[f"yst{ys}"], writes=[f"bncB{cch}_{sb % 2}_{h}"], dma_key=f"yst{ys}")
        if sb % 2 == 1:
            cch = sb // 2
            rd = [f"bncB{cch}_{s}_{h}" for s in range(2) for h in range(2)]
            cc_evs.append(P.op("pool", I("collective_compute", "AllGather", ALU.bypass, replica_groups=GROUPS,
                                         ins=[bounce.ap()[cch]], outs=[gath.ap()[cch * 1536:(cch + 1) * 1536, :]]),
                               reads=rd, writes=[f"gath{cch}"], dma_key=f"cc{cch}", inc=1))

    P.barrier()

    AR.reset(base_mark)
    mixed = AR.alloc(BF16, 16, 1024)
    p2_mark = AR.mark()
    fe_alloc()
    hTo = AR.alloc(BF16, 16, 1024)
    AT = AR.alloc(BF16, 12, 1024)
    wgs = [[AR.alloc(BF16, 16, 128) for _ in range(2)] for _ in range(2)]
    wbas = [AR.alloc(BF16, 4, 128) for _ in range(2)]
    wbbs = [AR.alloc(BF16, 8, 128) for _ in range(2)]
    sig = [[AR.alloc(F32, 512) for _ in range(2)] for _ in range(2)]
    tmp = [[AR.alloc(F32, 512) for _ in range(2)] for _ in range(2)]

    for c in range(12):
        P.op("pool", I("indirect_dma_start", out=AT[:, c, :], out_offset=None, in_=gath.ap(),
                       in_offset=bass.IndirectOffsetOnAxis(ap=idx[:, c:c + 1], axis=0)),
             reads=["idx"], writes=[f"AT{c}"], dma_key=f"at{c % 4}", extra=cc_evs)

    def load_fc(fc):
        s = fc % 2
        P.op("pool", I("dma_start", out=wgs[0][s], in_=wga.ap()[fc]), writes=[f"wga{s}"], dma_key=f"wga{s}")
        P.op("pool", I("dma_start", out=wgs[1][s], in_=wgb.ap()[fc]), writes=[f"wgb{s}"], dma_key=f"wgb{s}")
        P.op("pool", I("dma_start", out=wbas[s], in_=wba.ap()[fc]), writes=[f"wba{s}"], dma_key=f"wba{s}")
        P.op("pool", I("dma_start", out=wbbs[s], in_=wbb.ap()[fc]), writes=[f"wbb{s}"], dma_key=f"wbb{s}")

    load_fc(0)
    xo = xown.ap()
    for i in range(8):
        frontend_tile(xo[i * 128:(i + 1) * 128, :],
                      (lambda i=i: lambda half: hTo[:, half * 8:(half + 1) * 8, i * 128:(i + 1) * 128])(),
                      [f"hTo_{i}_0", f"hTo_{i}_1"], (0, 1))
    hTo_names = [[f"hTo_{i}_{h}" for i in range(4 * th, 4 * th + 4) for h in range(2)] for th in range(2)]
    at_chunk_a = [0, 3, 6, 9]
    at_chunk_b = [1, 2, 4, 5, 7, 8, 10, 11]
    it = 0
    for fc in range(16):
        s = fc % 2
        if fc + 1 < 16:
            load_fc(fc + 1)
        for th in range(2):
            k = it % 2
            it += 1
            GA, GB, YA, YB = 4 * k, 4 * k + 1, 4 * k + 2, 4 * k + 3
            tsl = slice(th * 512, th * 512 + 512)
            for kind, bi in ((0, GA), (1, GB)):
                wt = wgs[kind][s]
                wname = ("wga%d" if kind == 0 else "wgb%d") % s
                for c in range(16):
                    P.op("pe", I("matmul", bk32[bi], lhsT=wt[:, c, :], rhs=hTo[:, c, tsl], start=(c == 0), stop=(c == 15)),
                         reads=hTo_names[th] + [wname], writes=[BK[bi]])
                P.op("act", ACT(AF.Sigmoid, sig[kind][k], bk32[bi]), reads=[BK[bi]], writes=[f"sig{kind}_{k}"])
            for q in range(4):
                P.op("pe", I("matmul", bk32[YA], lhsT=wbas[s][:, q, :], rhs=AT[:, at_chunk_a[q], tsl],
                             start=(q == 0), stop=(q == 3)), reads=[f"wba{s}", f"AT{at_chunk_a[q]}"], writes=[BK[YA]])
            for q in range(8):
                P.op("pe", I("matmul", bk32[YB], lhsT=wbbs[s][:, q, :], rhs=AT[:, at_chunk_b[q], tsl],
                             start=(q == 0), stop=(q == 7)), reads=[f"wbb{s}", f"AT{at_chunk_b[q]}"], writes=[BK[YB]])
            P.op("dve", I("tensor_tensor", out=tmp[0][k], in0=bk32[YA], in1=sig[0][k], op=ALU.mult),
                 reads=[BK[YA], f"sig0_{k}"], writes=[f"tmp0_{k}"])
            P.op("dve", I("tensor_tensor", out=tmp[1][k], in0=bk32[YB], in1=sig[1][k], op=ALU.mult),
                 reads=[BK[YB], f"sig1_{k}"], writes=[f"tmp1_{k}"])
            P.op("dve", I("tensor_tensor", out=mixed[:, fc, tsl], in0=tmp[0][k], in1=tmp[1][k], op=ALU.add),
                 reads=[f"tmp0_{k}", f"tmp1_{k}"], writes=[f"mixed{fc}_{th}"])
    if debug:
        P.op("sp", I("dma_start", out=dbg_at.ap().rearrange("(c p) t -> p c t", p=128), in_=AT),
             reads=[f"AT{c}" for c in range(12)], writes=["dbg_at"], dma_key="dbg0")

    P.barrier()

    AR.reset(p2_mark)
    resid = AR.alloc(F32, 8, D_MODEL)
    res_mark = AR.mark()
    wos = [AR.alloc(BF16, 16, 512) for _ in range(2)]
    RES = [[f"res{i}_{nb}" for nb in range(4)] for i in range(8)]

    for i in range(8):
        P.op("sp", I("dma_start", out=resid[:, i, :], in_=xo[i * 128:(i + 1) * 128, :]), writes=RES[i], dma_key=f"res{i % 4}")

    def load_wo(nb):
        s = nb % 2
        for h2 in range(2):
            P.op("pool", I("dma_start", out=wos[s][:, 8 * h2:8 * h2 + 8, :], in_=wo.ap()[nb][:, 8 * h2:8 * h2 + 8, :]),
                 writes=[f"wo{s}"], dma_key=f"wo{s}_{h2}")

    load_wo(0)
    it = 0
    for nb in range(4):
        s = nb % 2
        if nb + 1 < 4:
            load_wo(nb + 1)
        for i in range(8):
            bi = it % 4
            it += 1
            for c in range(16):
                P.op("pe", I("matmul", bk32[bi], lhsT=mixed[:, c, i * 128:(i + 1) * 128], rhs=wos[s][:, c, :],
                             start=(c == 0), stop=(c == 15)), reads=[f"wo{s}"], writes=[BK[bi]])
            rsl = resid[:, i, nb * 512:(nb + 1) * 512]
            P.op("dve", I("tensor_tensor", out=rsl, in0=bk32[bi], in1=rsl, op=ALU.add),
                 reads=[BK[bi], RES[i][nb]], writes=[RES[i][nb]])
    if debug:
        P.op("sp", I("dma_start", out=dbg_x1.ap().rearrange("(i p) n -> p i n", p=128), in_=resid),
             reads=[r for rr_ in RES for r in rr_], writes=["dbg_x1"], dma_key="dbg1")

    P.op("sp", I("dma_start", out=gb, in_=gmlpb.ap()), reads=[], writes=["gb"], dma_key="c2")
    P.barrier()
    AR.reset(base_mark)
    wds = [AR.alloc(BF16, 4, 2048) for _ in range(2)]
    assert AR.mark() == p2_mark
    AR.reset(res_mark)
    fe_alloc(xs=False)
    h2T = AR.alloc(BF16, 16, 1024)
    wus = [AR.alloc(BF16, 16, 512) for _ in range(2)]
    uTs = [AR.alloc(BF16, 4, 1024) for _ in range(2)]
    rl = [AR.alloc(F32, 512) for _ in range(2)]

    def load_hg(hg):
        s = hg % 2
        for h2 in range(2):
            P.op("pool", I("dma_start", out=wus[s][:, 8 * h2:8 * h2 + 8, :], in_=wu.ap()[hg][:, 8 * h2:8 * h2 + 8, :]),
                 writes=[f"wu{s}"], dma_key=f"wu{s}_{h2}")
        for h2 in range(2):
            P.op("pool", I("dma_start", out=wds[s][:, 2 * h2:2 * h2 + 2, :], in_=wd.ap()[hg][:, 2 * h2:2 * h2 + 2, :]),
                 writes=[f"wd{s}"], dma_key=f"wd{s}_{h2}")

    load_hg(0)
    for i in range(8):
        frontend_tile(resid[:, i, :],
                      (lambda i=i: lambda half: h2T[:, half * 8:(half + 1) * 8, i * 128:(i + 1) * 128])(),
                      [f"h2T_{i}_0", f"h2T_{i}_1"], (0, 1), xs_res=RES[i])
    h2T_names = [[f"h2T_{i}_{h}" for i in range(4 * th, 4 * th + 4) for h in range(2)] for th in range(2)]
    ucnt = 0
    dcnt = 0
    for hg in range(16):
        s = hg % 2
        if hg + 1 < 16:
            load_hg(hg + 1)
        uT = uTs[s]
        for th in range(2):
            tsl = slice(th * 512, th * 512 + 512)
            for hc in range(4):
                bi = ucnt % 3
                k = ucnt % 2
                ucnt += 1
                for c in range(16):
                    P.op("pe", I("matmul", bk32[bi], lhsT=wus[s][:, c, hc * 128:(hc + 1) * 128], rhs=h2T[:, c, tsl],
                                 start=(c == 0), stop=(c == 15)), reads=h2T_names[th] + [f"wu{s}"], writes=[BK[bi]])
                P.op("act", ACT(AF.Relu, rl[k], bk32[bi]), reads=[BK[bi]], writes=[f"rl{k}"])
                P.op("dve", I("tensor_tensor", out=uT[:, hc, tsl], in0=rl[k], in1=rl[k], op=ALU.mult),
                     reads=[f"rl{k}"], writes=[f"uT{s}_{hc}_{th}"])
        for i in range(8):
            th = i // 4
            for nb in range(4):
                bi = 3 + dcnt % 5
                dcnt += 1
                for hc in range(4):
                    P.op("pe", I("matmul", bk32[bi], lhsT=uT[:, hc, i * 128:(i + 1) * 128],
                                 rhs=wds[s][:, hc, nb * 512:(nb + 1) * 512], start=(hc == 0), stop=(hc == 3)),
                         reads=[f"uT{s}_{hc}_{th}", f"wd{s}"], writes=[BK[bi]])
                rsl = resid[:, i, nb * 512:(nb + 1) * 512]
                P.op("dve", I("tensor_tensor", out=rsl, in0=bk32[bi], in1=rsl, op=ALU.add),
                     reads=[BK[bi], RES[i][nb]], writes=[RES[i][nb]])
    outs = []
    for i in range(8):
        outs.append(P.op("sp", I("dma_start", out=out.ap()[i * 128:(i + 1) * 128, :], in_=resid[:, i, :]),
                         reads=RES[i], writes=[f"out{i}"], dma_key=f"out{i % 4}"))
    P.op("sp", None, extra=outs)
    P.barrier()

    with nc.Block() as block:
        P.emit(block)
    return nc


def _rel_bucket(dist):
    d = np.maximum(dist, 0)
    df = np.maximum(d, 1).astype(np.float32)
    large = 16 + (np.log(df / np.float32(16)) / np.float32(np.log(2048 / 16)) * np.float32(16)).astype(np.int32)
    large = np.minimum(large, 31)
    return np.where(d < 16, d, large)


def _bias_tables(rel_bias, heads_a, heads_b):
    a = np.arange(128)
    ap_, aq = a[:, None], a[None, :]
    negf = np.float32(NEG)

    def pat(head, dist, valid):
        v = rel_bias[_rel_bucket(dist), head].astype(np.float32)
        return np.where(valid, v, negf).astype(np.float32)

    blocks = []
    for D in range(2):
        for rp in range(4):
            cols = []
            for rr in range(4):
                dist = 512 * D + 4 * (aq - ap_) + (rr - rp)
                cols.append(pat(heads_a[0], dist, (dist >= 0) & (dist <= 128)))
            blocks.append(np.concatenate(cols, axis=1))
    for D in range(2):
        dist = 512 * D + 4 * (aq - ap_)
        blocks.append(pat(heads_a[1], dist, (dist >= 0) & (dist <= 512)))
    for m in range(5):
        dist = 512 * m + 4 * (aq - ap_)
        blocks.append(pat(heads_a[2], dist, (dist >= 0) & (dist % 16 == 0) & (dist <= 2048)))
    biasA = np.ascontiguousarray(np.concatenate(blocks, axis=1))
    assert biasA.shape == (128, 4992)
    cc = np.arange(4096)[None, :]
    tt = []
    for hb in heads_b:
        dist = cc - ap_
        tt.append(pat(hb, dist, dist >= 0))
    ttB = np.ascontiguousarray(np.concatenate(tt, axis=1))
    return biasA, ttB


def _prep_inputs(x, g_mix, w_in, q_norm_a, k_norm_a, q_norm_b, k_norm_b, rel_bias,
                 w_branch_a, w_branch_b, w_out, g_mlp, w_up, w_down):
    f = lambda a: np.ascontiguousarray(np.asarray(a, dtype=np.float32))
    x, w_in, rel_bias = f(x), f(w_in), f(rel_bias)
    shared = {}
    shared["gmixb"] = np.ascontiguousarray(np.broadcast_to(f(g_mix)[None, :], (128, D_MODEL)))
    shared["gmlpb"] = np.ascontiguousarray(np.broadcast_to(f(g_mlp)[None, :], (128, D_MODEL)))
    shared["qkg"] = np.ascontiguousarray(np.stack([f(q_norm_a), f(k_norm_a), f(q_norm_b), f(k_norm_b)], axis=1))
    shared["identd"] = np.eye(128, dtype=np.float32)
    em = np.zeros((16, 16, 128), np.float32)
    for n in range(16):
        em[n, n, :] = 1.0
    shared["ematd"] = em.reshape(16, 16 * 128)
    wg = w_in[:, A_QKV + B_QKV:]
    lay = lambda w, nf: np.ascontiguousarray(w.reshape(16, 128, nf, 128).transpose(2, 1, 0, 3))
    shared["wga"] = lay(wg[:, :D_MODEL], 16)
    shared["wgb"] = lay(wg[:, D_MODEL:], 16)
    shared["wba"] = np.ascontiguousarray(f(w_branch_a).reshape(4, 128, 16, 128).transpose(2, 1, 0, 3))
    shared["wbb"] = np.ascontiguousarray(f(w_branch_b).reshape(8, 128, 16, 128).transpose(2, 1, 0, 3))
    shared["wo"] = np.ascontiguousarray(f(w_out).reshape(16, 128, 4, 512).transpose(2, 1, 0, 3))
    shared["wu"] = np.ascontiguousarray(f(w_up).reshape(16, 128, 16, 512).transpose(2, 1, 0, 3))
    shared["wd"] = np.ascontiguousarray(f(w_down).reshape(16, 4, 128, 2048).transpose(0, 2, 1, 3))
    in_maps = []
    for c in range(8):
        b, j = divmod(c, 4)
        m = dict(shared)
        m["xfull"] = np.ascontiguousarray(x[b])
        m["xown"] = np.ascontiguousarray(x[b, 1024 * j:1024 * (j + 1)])
        heads_a = [j, 4 + j, 8 + j]
        cols = []
        for h in heads_a:
            for t in range(3):
                cols.append(w_in[:, t * 1536 + h * 128: t * 1536 + (h + 1) * 128])
        wa = np.concatenate(cols, axis=1)
        m["wA"] = np.ascontiguousarray(wa.reshape(16, 128, 1152).transpose(1, 0, 2))
        heads_b = [2 * j, 2 * j + 1]
        cols = []
        for h in heads_b:
            for t in range(3):
                cols.append(w_in[:, A_QKV + t * 1024 + h * 128: A_QKV + t * 1024 + (h + 1) * 128])
        wb = np.concatenate(cols, axis=1)
        m["wB"] = np.ascontiguousarray(wb.reshape(16, 128, 768).transpose(1, 0, 2))
        m["biasA"], m["ttB"] = _bias_tables(rel_bias, heads_a, [A_HEADS + h for h in heads_b])
        m["idxd"] = (j * 1536 + np.arange(12)[None, :] * 128 + np.arange(128)[:, None]).astype(np.uint32)
        in_maps.append(m)
    return in_maps


_CACHE = {}


def kernel(**inputs):
    debug = bool(os.environ.get("MK_DEBUG"))
    key = ("nc", debug)
    if key not in _CACHE:
        _CACHE[key] = build_program(debug)
    nc = _CACHE[key]
    in_maps = _prep_inputs(**inputs)
    res = run_bass_kernel_spmd(nc, in_maps, core_ids=list(range(8)))
    outs = [np.asarray(r["out"], dtype=np.float32) for r in res.results]
    full = np.stack([np.concatenate(outs[0:4], axis=0), np.concatenate(outs[4:8], axis=0)], axis=0)
    if debug:
        kernel.debug = res.results
    return full
```

```python
import os
import numpy as np
import concourse.bass as bass
import concourse.mybir as mybir
from concourse.bass_utils import run_bass_kernel_spmd

F32 = mybir.dt.float32
BF16 = mybir.dt.bfloat16
U32 = mybir.dt.uint32
AF = mybir.ActivationFunctionType
ALU = mybir.AluOpType
AX = mybir.AxisListType

D_MODEL = 2048
SEQ = 4096
HD = 128
A_HEADS = 12
A_QKV = 3 * A_HEADS * HD
B_WIDTH = 1024
B_QKV = 3 * B_WIDTH
EPS = 1e-6
SCALE = HD ** -0.5
NEG = -30000.0
GROUPS = [[0, 1, 2, 3], [4, 5, 6, 7]]
ARENA_BYTES = 204 * 1024


class Ev:
    __slots__ = ("eng", "idx", "fn", "deps", "signal", "kind", "sem", "val", "sigcount", "inc")

    def __init__(self, eng, idx, fn):
        self.eng = eng
        self.idx = idx
        self.fn = fn
        self.deps = set()
        self.signal = False
        self.kind = "compute"
        self.sem = None
        self.val = 0
        self.sigcount = 0
        self.inc = 0


class Prog:
    ENGS = ("pe", "act", "dve", "pool", "sp")

    def __init__(self, nc):
        self.nc = nc
        self.ops = {e: [] for e in self.ENGS}
        self.last_w = {}
        self.readers = {}
        self.dma = {}
        self.psem = {e: nc.alloc_semaphore("prog_" + e) for e in self.ENGS}

    def _dep(self, ev, d):
        if d is ev or d is None:
            return
        ev.deps.add(d)
        if d.kind == "compute" and not (d.eng == "pe" and ev.eng == "pe"):
            d.signal = True

    def op(self, eng, fn, reads=(), writes=(), dma_key=None, inc=16, extra=()):
        ev = Ev(eng, len(self.ops[eng]), fn)
        for d in extra:
            self._dep(ev, d)
        for r in reads:
            self._dep(ev, self.last_w.get(r))
        for w in writes:
            self._dep(ev, self.last_w.get(w))
            for rd in self.readers.get(w, ()):
                self._dep(ev, rd)
        if dma_key is not None:
            st = self.dma.get(dma_key)
            if st is None:
                st = [self.nc.alloc_semaphore("dma_" + dma_key), 0, None]
                self.dma[dma_key] = st
            self._dep(ev, st[2])
            ev.kind = "dma"
            ev.sem = st[0]
            ev.inc = inc
            st[1] += inc
            ev.val = st[1]
            st[2] = ev
        for r in reads:
            self.readers.setdefault(r, []).append(ev)
        for w in writes:
            self.last_w[w] = ev
            self.readers[w] = []
        self.ops[eng].append(ev)
        return ev

    def barrier(self):
        lasts = []
        for e in self.ENGS:
            for ev in reversed(self.ops[e]):
                if ev.kind == "compute" and ev.fn is not None:
                    lasts.append(ev)
                    break
        dmas = [st[2] for st in self.dma.values() if st[2] is not None]
        for e in self.ENGS:
            self.op(e, None, extra=lasts + dmas)
        self.last_w = {}
        self.readers = {}

    def emit(self, block):
        for e in self.ENGS:
            c = 0
            for ev in self.ops[e]:
                if ev.kind == "compute" and ev.signal:
                    c += 1
                    ev.sigcount = c
                else:
                    ev.sigcount = -1
        for e in self.ENGS:
            for i, ev in enumerate(self.ops[e]):
                if ev.kind == "compute" and ev.signal and ev.fn is None:
                    raise RuntimeError("wait-only op cannot signal")

        def run(ename, eng):
            waited = {}
            for ev in self.ops[ename]:
                needs = {}
                for d in ev.deps:
                    if d.kind == "dma":
                        sem, val = d.sem, d.val
                    else:
                        if d.eng == "pe" and ename == "pe":
                            continue
                        sem, val = self.psem[d.eng], d.sigcount
                        assert val > 0
                    k = id(sem)
                    if k not in needs or needs[k][1] < val:
                        needs[k] = (sem, val)
                for k, (sem, val) in needs.items():
                    if waited.get(k, 0) < val:
                        eng.wait_ge(sem, val)
                        waited[k] = val
                if ev.fn is None:
                    continue
                ins = ev.fn(eng)
                if ev.kind == "dma":
                    ins.then_inc(ev.sem, ev.inc)
                elif ev.signal:
                    ins.then_inc(self.psem[ename], 1)

        block.tensor(lambda eng: run("pe", eng))
        block.scalar(lambda eng: run("act", eng))
        block.vector(lambda eng: run("dve", eng))
        block.gpsimd(lambda eng: run("pool", eng))
        block.sync(lambda eng: run("sp", eng))


class Arena:
    def __init__(self, nc, nbytes):
        self.t = nc.alloc_sbuf_tensor("arena", [128, nbytes // 2], BF16)
        self.nbytes = nbytes
        self.top = 0

    def mark(self):
        return self.top

    def reset(self, m):
        self.top = m

    def alloc(self, dtype, *dims):
        esz = 4 if dtype in (F32, U32) else 2
        n = 1
        for d in dims:
            n *= d
        nb = (n * esz + 63) // 64 * 64
        off = self.top
        self.top += nb
        assert self.top <= self.nbytes, f"arena overflow {self.top} > {self.nbytes}"
        v = self.t[:, off // 2: off // 2 + (n * esz) // 2]
        if esz == 4:
            v = v.bitcast(dtype)
        if len(dims) == 2:
            v = v.rearrange("p (a b) -> p a b", a=dims[0])
        elif len(dims) == 3:
            v = v.rearrange("p (a b c) -> p a b c", a=dims[0], b=dims[1])
        return v


def I(name, *args, **kw):
    return lambda e: getattr(e, name)(*args, **kw)


def build_program(debug=False):
    nc = bass.Bass("TRN2", target_bir_lowering=False)
    din = lambda name, shape, dt=F32: nc.dram_tensor(name, shape, dt, kind="ExternalInput")
    xfull = din("xfull", [SEQ, D_MODEL])
    xown = din("xown", [1024, D_MODEL])
    gmixb = din("gmixb", [128, D_MODEL])
    gmlpb = din("gmlpb", [128, D_MODEL])
    wA = din("wA", [128, 16, 1152])
    wB = din("wB", [128, 16, 768])
    qkg = din("qkg", [128, 4])
    biasA = din("biasA", [128, 4992])
    ttB = din("ttB", [128, 2 * 4096])
    identd = din("identd", [128, 128])
    ematd = din("ematd", [128, 16 * 128])
    wga = din("wga", [16, 128, 16, 128])
    wgb = din("wgb", [16, 128, 16, 128])
    wba = din("wba", [16, 128, 4, 128])
    wbb = din("wbb", [16, 128, 8, 128])
    wo = din("wo", [4, 128, 16, 512])
    wu = din("wu", [16, 128, 16, 512])
    wd = din("wd", [16, 128, 4, 2048])
    idxd = din("idxd", [128, 12], U32)
    out = nc.dram_tensor("out", [1024, D_MODEL], F32, kind="ExternalOutput")
    bounce = nc.dram_tensor("bounce", [4, 384, 1024], BF16)
    gath = nc.dram_tensor("gath", [4 * 1536, 1024], BF16)
    if debug:
        dbg_at = nc.dram_tensor("dbg_at", [1536, 1024], BF16, kind="ExternalOutput")
        dbg_x1 = nc.dram_tensor("dbg_x1", [1024, D_MODEL], F32, kind="ExternalOutput")

    P = Prog(nc)
    AR = Arena(nc, ARENA_BYTES)
    banks = [nc.alloc_psum_tensor(f"bank{i}", [128, 512], F32) for i in range(8)]
    bk32 = [b[:, :] for b in banks]
    bk16 = [b[:, :].bitcast(BF16) for b in banks]
    BK = [f"bank{i}" for i in range(8)]

    ident = AR.alloc(BF16, 128)
    ones = AR.alloc(BF16, 128)
    zeros = AR.alloc(BF16, 512)
    gb = AR.alloc(F32, D_MODEL)
    qk4 = AR.alloc(F32, 4)
    gsc = AR.alloc(F32, 2)
    small = AR.alloc(F32, 64)
    em = AR.alloc(BF16, 16, 128)
    idx = AR.alloc(U32, 12)
    base_mark = AR.mark()

    def ACT(func, out_, in_, scale=1.0, bias=0.0, accum=None):
        if accum is None:
            return I("activation", out=out_, in_=in_, func=func, bias=bias, scale=scale)
        return I("activation", out=out_, in_=in_, func=func, bias=bias, scale=scale, accum_out=accum)

    P.op("pool", I("dma_start", out=ident, in_=identd.ap()), writes=["ident"], dma_key="c0")
    P.op("pool", I("dma_start", out=em, in_=ematd.ap().rearrange("p (a b) -> p a b", a=16)), writes=["em"], dma_key="c1")
    P.op("sp", I("dma_start", out=gb, in_=gmixb.ap()), writes=["gb"], dma_key="c2")
    P.op("sp", I("dma_start", out=qk4, in_=qkg.ap()), writes=["qk4"], dma_key="c3")
    P.op("sp", I("dma_start", out=idx, in_=idxd.ap()), writes=["idx"], dma_key="c4")
    P.op("dve", I("memset", ones, 1.0), writes=["ones"])
    P.op("dve", I("memset", zeros, 0.0), writes=["zeros"])
    P.op("dve", I("tensor_scalar", out=gsc[:, 0:1], in0=qk4[:, 0:1], scalar1=qk4[:, 1:2], scalar2=SCALE,
                  op0=ALU.mult, op1=ALU.mult), reads=["qk4"], writes=["gsc0"])
    P.op("dve", I("tensor_scalar", out=gsc[:, 1:2], in0=qk4[:, 2:3], scalar1=qk4[:, 3:4], scalar2=SCALE,
                  op0=ALU.mult, op1=ALU.mult), reads=["qk4"], writes=["gsc1"])

    fe = {"cnt": 0}

    def fe_alloc(xs=True):
        fe["xs"] = [AR.alloc(F32, D_MODEL) for _ in range(2)] if xs else None
        fe["hb"] = [AR.alloc(BF16, D_MODEL) for _ in range(2)]

    def frontend_tile(src_ap, dst_fn, dst_names, tb, xs_res=None):
        t = fe["cnt"]
        fe["cnt"] += 1
        s = t % 2
        hb = fe["hb"][s]
        ss = small[:, 2 * s: 2 * s + 1]
        lt = small[:, 4 + 2 * s: 5 + 2 * s]
        rs = small[:, 8 + 2 * s: 9 + 2 * s]
        if xs_res is None:
            xs = fe["xs"][s]
            xr = [f"xs{s}"]
            P.op("sp", I("dma_start", out=xs, in_=src_ap), writes=xr, dma_key=f"xs{s}")
        else:
            xs = src_ap
            xr = list(xs_res)
        P.op("act", ACT(AF.Square, hb, xs, accum=ss), reads=xr, writes=[f"hb{s}", f"ss{s}"])
        P.op("act", ACT(AF.Ln, lt, ss, scale=1.0 / D_MODEL, bias=EPS), reads=[f"ss{s}"], writes=[f"lt{s}"])
        P.op("act", ACT(AF.Exp, rs, lt, scale=-0.5), reads=[f"lt{s}"], writes=[f"rs{s}"])
        P.op("dve", I("scalar_tensor_tensor", out=hb, in0=xs, scalar=rs, in1=gb, op0=ALU.mult, op1=ALU.mult),
             reads=xr + [f"rs{s}", "gb"], writes=[f"hb{s}"])
        for half in range(2):
            bi = tb[half]
            for c8 in range(8):
                c = half * 8 + c8
                P.op("pe", I("transpose", out=bk16[bi][:, c8 * 128:(c8 + 1) * 128], in_=hb[:, c * 128:(c + 1) * 128],
                             identity=ident), reads=[f"hb{s}", "ident"], writes=[BK[bi]])
            src = bk16[bi].rearrange("p (a b) -> p a b", a=8)
            dst = dst_fn(half)
            if half == 0:
                P.op("act", I("activation", out=dst, in_=src, func=AF.Copy), reads=[BK[bi]], writes=[dst_names[half]])
            else:
                P.op("dve", I("tensor_copy", out=dst, in_=src), reads=[BK[bi]], writes=[dst_names[half]])

    def project(W, cb, hT, hT_names, bi):
        for c in range(16):
            P.op("pe", I("matmul", bk32[bi], lhsT=W[:, c, cb * 128:(cb + 1) * 128], rhs=hT[:, c, :],
                         start=(c == 0), stop=(c == 15)), reads=hT_names + ["W"], writes=[BK[bi]])

    pp = {"cnt": 0}

    def pp_alloc():
        pp["sq"] = [AR.alloc(BF16, 512) for _ in range(2)]
        pp["lt"] = [AR.alloc(F32, 512) for _ in range(2)]
        pp["rs"] = [AR.alloc(F32, 512) for _ in range(2)]

    def qk_post(bi, sbank, dst, dst_name, gcol, perm):
        k = pp["cnt"] % 2
        pp["cnt"] += 1
        sq, lt, rs = pp["sq"][k], pp["lt"][k], pp["rs"][k]
        src = bk32[bi]
        P.op("act", ACT(AF.Square, sq, src), reads=[BK[bi]], writes=[f"sq{k}"])
        P.op("pe", I("matmul", bk32[sbank], lhsT=ones, rhs=sq, start=True, stop=True),
             reads=[f"sq{k}", "ones"], writes=[BK[sbank]])
        P.op("act", ACT(AF.Ln, lt, bk32[sbank], scale=1.0 / HD, bias=EPS), reads=[BK[sbank]], writes=[f"plt{k}"])
        P.op("act", ACT(AF.Exp, rs, lt, scale=-0.5), reads=[f"plt{k}"], writes=[f"prs{k}"])
        if perm:
            a0 = src.rearrange("p (a r) -> p r a", r=4)
            a1 = rs.rearrange("p (a r) -> p r a", r=4)
            o = dst.rearrange("p (r a) -> p r a", r=4)
        else:
            a0, a1, o = src, rs, dst
        if gcol is None:
            P.op("dve", I("tensor_tensor", out=o, in0=a0, in1=a1, op=ALU.mult),
                 reads=[BK[bi], f"prs{k}"], writes=[dst_name])
        else:
            P.op("dve", I("scalar_tensor_tensor", out=o, in0=a0, scalar=gsc[:, gcol:gcol + 1], in1=a1,
                          op0=ALU.mult, op1=ALU.mult), reads=[BK[bi], f"prs{k}", f"gsc{gcol}"], writes=[dst_name])

    def finalize_tile(obank, ocol, ybuf, yname, tbank, tcol, k):
        rec = small[:, 16 + k: 17 + k]
        P.op("dve", I("reciprocal", out=rec, in_=bk32[obank][:, ocol + 128: ocol + 129]),
             reads=[BK[obank]], writes=[f"rec{k}"])
        P.op("act", I("activation", out=ybuf, in_=bk32[obank][:, ocol: ocol + 128], func=AF.Copy, scale=rec),
             reads=[BK[obank], f"rec{k}"], writes=[yname])
        P.op("pe", I("transpose", out=bk16[tbank][:, tcol: tcol + 128], in_=ybuf, identity=ident),
             reads=[yname, "ident"], writes=[BK[tbank]])

    def zero_init(ob):
        P.op("pe", I("matmul", bk32[ob][:, 0:258], lhsT=zeros[:, 0:128], rhs=zeros[:, 0:258],
                     start=True, stop=False, skip_group_check=True), reads=["zeros"], writes=[BK[ob]])

    T0, T1, P0, P1, A0, A1, O0, O1 = range(8)
    xf = xfull.ap()
    cnt = {"p": 0, "a": 0, "pt": 0, "y": 0, "g": 0}

    def nxt(k, mod):
        v = cnt[k] % mod
        cnt[k] += 1
        return v

    def cast_load(dst, src, name, key, chunk):
        P.op("pool", I("dma_start", out=dst.rearrange("p (a b) -> p a b", b=chunk),
                       in_=src.rearrange("p (a b) -> p a b", b=chunk)), writes=[name], dma_key=key)

    def merge(lists):
        lists = [l for l in lists if l]
        pos = [0] * len(lists)
        for _ in range(sum(len(l) for l in lists)):
            best = min((pos[i] / len(lists[i]), i) for i in range(len(lists)) if pos[i] < len(lists[i]))[1]
            lists[best][pos[best]]()
            pos[best] += 1

    cc_evs = []

    def run_pass(kind):
        isA = kind == "A"
        nh = 3 if isA else 2
        AR.reset(base_mark)
        fe_alloc()
        pp_alloc()
        W = AR.alloc(BF16, 16, 384 * nh)
        KT = AR.alloc(BF16, nh, SEQ)
        Vst = AR.alloc(BF16, nh * 32, 130)
        hTs = [AR.alloc(BF16, 16, 512) for _ in range(2)]
        QTs = [AR.alloc(BF16, nh, 512) for _ in range(2)]
        VTs = [AR.alloc(BF16, 512) for _ in range(2)]
        PTs = [AR.alloc(BF16, 512) for _ in range(3)]
        ybufs = [AR.alloc(BF16, 128) for _ in range(4)]
        ystage = [AR.alloc(BF16, 512) for _ in range(2)]
        wsrc = wA if isA else wB
        for c4 in range(4):
            P.op("pool", I("dma_start", out=W[:, 4 * c4:4 * c4 + 4, :], in_=wsrc.ap()[:, 4 * c4:4 * c4 + 4, :]),
                 writes=["W"], dma_key=f"w{c4}")
        if isA:
            bA = AR.alloc(BF16, 4992)
            cast_load(bA, biasA.ap(), "bA", "c5", 128)
        else:
            TT = AR.alloc(BF16, 2, 4096)
            KMf = AR.alloc(F32, 2, 16)
            KMs = AR.alloc(F32, 2, 16)
            KMh = AR.alloc(BF16, 2, 16)
            KMl = AR.alloc(BF16, 2, 16)
            g16 = [AR.alloc(F32, 16) for _ in range(4)]
            m8 = [AR.alloc(F32, 8) for _ in range(4)]
            thr = [AR.alloc(F32, 1) for _ in range(4)]
            negm = [AR.alloc(BF16, 128) for _ in range(4)]
            negT = [[AR.alloc(BF16, 512) for _ in range(2)] for _ in range(2)]
            cast_load(TT.rearrange("p a b -> p (a b)"), ttB.ap(), "TT", "c5", 1024)
            for k in range(4):
                P.op("dve", I("memset", g16[k], -1e30), writes=[f"g16_{k}"])
                P.op("dve", I("memset", negm[k], 0.0), writes=[f"negm{k}"])
            P.op("dve", I("memset", KMh, 0.0), writes=["KMh"])
            P.op("dve", I("memset", KMl, 0.0), writes=["KMl"])
        P.op("dve", I("memset", Vst[:, :, 128:130], 1.0), writes=["vones"])

        def fe_steps(sb):
            hs = sb % 2
            hT = hTs[hs]
            st = {}

            def pre(i):
                def f():
                    t = fe["cnt"]
                    fe["cnt"] += 1
                    s = t % 2
                    st[i] = s
                    xs, hb = fe["xs"][s], fe["hb"][s]
                    ss = small[:, 2 * s: 2 * s + 1]
                    lt = small[:, 4 + 2 * s: 5 + 2 * s]
                    rs = small[:, 8 + 2 * s: 9 + 2 * s]
                    r0 = sb * 512 + i * 128
                    P.op("sp", I("dma_start", out=xs, in_=xf[r0:r0 + 128, :]), writes=[f"xs{s}"], dma_key=f"xs{s}")
                    P.op("act", ACT(AF.Square, hb, xs, accum=ss), reads=[f"xs{s}"], writes=[f"hb{s}", f"ss{s}"])
                    P.op("act", ACT(AF.Ln, lt, ss, scale=1.0 / D_MODEL, bias=EPS), reads=[f"ss{s}"], writes=[f"lt{s}"])
                    P.op("act", ACT(AF.Exp, rs, lt, scale=-0.5), reads=[f"lt{s}"], writes=[f"rs{s}"])
                    P.op("dve", I("scalar_tensor_tensor", out=hb, in0=xs, scalar=rs, in1=gb, op0=ALU.mult, op1=ALU.mult),
                         reads=[f"xs{s}", f"rs{s}", "gb"], writes=[f"hb{s}"])
                return f

            def pe(i, half):
                def f():
                    s = st[i]
                    hb = fe["hb"][s]
                    for c8 in range(8):
                        c = half * 8 + c8
                        P.op("pe", I("transpose", out=bk16[T0][:, c8 * 128:(c8 + 1) * 128],
                                     in_=hb[:, c * 128:(c + 1) * 128], identity=ident),
                             reads=[f"hb{s}", "ident"], writes=[BK[T0]])
                    src = bk16[T0].rearrange("p (a b) -> p a b", a=8)
                    dst = hT[:, half * 8:(half + 1) * 8, i * 128:(i + 1) * 128]
                    if half == 0:
                        P.op("act", I("activation", out=dst, in_=src, func=AF.Copy), reads=[BK[T0]], writes=[f"hT{hs}_{i}_0"])
                    else:
                        P.op("dve", I("tensor_copy", out=dst, in_=src), reads=[BK[T0]], writes=[f"hT{hs}_{i}_1"])
                return f
            return [pre(0), pre(1), pe(0, 0), pe(0, 1), pre(2), pe(1, 0), pe(1, 1), pre(3),
                    pe(2, 0), pe(2, 1), pe(3, 0), pe(3, 1)]

        def proj_steps(sb):
            hs = sb % 2
            hT = hTs[hs]
            hT_names = [f"hT{hs}_{i}_{h}" for i in range(4) for h in range(2)]
            QT = QTs[hs]
            blocks = [(h, t) for h in range(nh) for t in range(3)]
            info = {}

            def s_pe(b, part):
                def f():
                    h, t = blocks[b]
                    if part == 0:
                        info[b] = {"k": nxt("p", 2)}
                    bi = (P0, P1)[info[b]["k"]]
                    for c in range(8 * part, 8 * part + 8):
                        P.op("pe", I("matmul", bk32[bi], lhsT=W[:, c, (3 * h + t) * 128:(3 * h + t + 1) * 128],
                                     rhs=hT[:, c, :], start=(c == 0), stop=(c == 15)), reads=hT_names + ["W"], writes=[BK[bi]])
                return f

            def s_mid(b):
                def f():
                    h, t = blocks[b]
                    k = info[b]["k"]
                    bi = (P0, P1)[k]
                    if t < 2:
                        kk = pp["cnt"] % 2
                        pp["cnt"] += 1
                        info[b]["kk"] = kk
                        P.op("act", ACT(AF.Square, pp["sq"][kk], bk32[bi]), reads=[BK[bi]], writes=[f"sq{kk}"])
                    else:
                        VT = VTs[k]
                        if isA:
                            P.op("act", I("activation", out=VT.rearrange("p (r a) -> p r a", r=4),
                                          in_=bk32[bi].rearrange("p (a r) -> p r a", r=4), func=AF.Copy),
                                 reads=[BK[bi]], writes=[f"VT{k}"])
                        else:
                            P.op("act", I("activation", out=VT, in_=bk32[bi], func=AF.Copy), reads=[BK[bi]], writes=[f"VT{k}"])
                return f

            def s_post(b):
                def f():
                    h, t = blocks[b]
                    k = info[b]["k"]
                    bi = (P0, P1)[k]
                    if t == 2:
                        VT = VTs[k]
                        for r in range(4):
                            P.op("pe", I("transpose", out=bk16[T1][:, r * 128:(r + 1) * 128], in_=VT[:, r * 128:(r + 1) * 128],
                                         identity=ident), reads=[f"VT{k}", "ident"], writes=[BK[T1]])
                        P.op("dve", I("tensor_copy", out=Vst[:, h * 32 + sb * 4: h * 32 + sb * 4 + 4, 0:128],
                                      in_=bk16[T1][:, 0:512].rearrange("p (r a) -> p r a", r=4)),
                             reads=[BK[T1]], writes=[f"V{h}_{sb}"])
                        return
                    kk = info[b]["kk"]
                    lt, rs = pp["lt"][kk], pp["rs"][kk]
                    P.op("pe", I("matmul", bk32[T1], lhsT=ones, rhs=pp["sq"][kk], start=True, stop=True),
                         reads=[f"sq{kk}", "ones"], writes=[BK[T1]])
                    P.op("act", ACT(AF.Ln, lt, bk32[T1], scale=1.0 / HD, bias=EPS), reads=[BK[T1]], writes=[f"plt{kk}"])
                    P.op("act", ACT(AF.Exp, rs, lt, scale=-0.5), reads=[f"plt{kk}"], writes=[f"prs{kk}"])
                    src = bk32[bi]
                    if t == 0:
                        dst, dname, gcol = QT[:, h, :], f"QT{hs}_{h}", None
                    else:
                        dst, dname, gcol = KT[:, h, sb * 512:(sb + 1) * 512], f"KT{h}_{sb}", (0 if isA else 1)
                    if isA:
                        a0 = src.rearrange("p (a r) -> p r a", r=4)
                        a1 = rs.rearrange("p (a r) -> p r a", r=4)
                        o = dst.rearrange("p (r a) -> p r a", r=4)
                    else:
                        a0, a1, o = src, rs, dst
                    if gcol is None:
                        P.op("dve", I("tensor_tensor", out=o, in0=a0, in1=a1, op=ALU.mult),
                             reads=[BK[bi], f"prs{kk}"], writes=[dname])
                    else:
                        P.op("dve", I("scalar_tensor_tensor", out=o, in0=a0, scalar=gsc[:, gcol:gcol + 1], in1=a1,
                                      op0=ALU.mult, op1=ALU.mult), reads=[BK[bi], f"prs{kk}", f"gsc{gcol}"], writes=[dname])
                    if (not isA) and t == 1:
                        sl = slice(2 * sb, 2 * sb + 2)
                        P.op("dve", I("tensor_reduce", out=KMf[:, h, sl],
                                      in_=KT[:, h, sb * 512:(sb + 1) * 512].rearrange("p (n t) -> p n t", n=2),
                                      axis=AX.X, op=ALU.add), reads=[f"KT{h}_{sb}"], writes=[f"KMf{h}"])
                        P.op("dve", I("tensor_scalar", out=KMs[:, h, sl], in0=KMf[:, h, sl], scalar1=1.0 / 256, scalar2=None,
                                      op0=ALU.mult), reads=[f"KMf{h}"], writes=[f"KMs{h}"])
                        P.op("dve", I("tensor_copy", out=KMh[:, h, sl], in_=KMs[:, h, sl]), reads=[f"KMs{h}"], writes=["KMh"])
                        P.op("dve", I("tensor_tensor", out=KMl[:, h, sl], in0=KMs[:, h, sl], in1=KMh[:, h, sl],
                                      op=ALU.subtract), reads=[f"KMs{h}", "KMh"], writes=["KMl"])
                return f

            def s_g1(h):
                def f():
                    for i in range(4):
                        nbq = (4 * sb + i) // 2
                        gcols = bk32[T1][:, 512 - 16 * (i + 1): 512 - 16 * i]
                        P.op("pe", I("matmul", gcols, lhsT=QT[:, h, i * 128:(i + 1) * 128], rhs=KMh[:, h, :],
                                     start=True, stop=False), reads=[f"QT{hs}_{h}", "KMh"], writes=[BK[T1]])
                        P.op("pe", I("matmul", gcols, lhsT=QT[:, h, i * 128:(i + 1) * 128], rhs=KMl[:, h, :],
                                     start=False, stop=True), reads=[f"QT{hs}_{h}", "KMl"], writes=[BK[T1]])
                    for i in range(4):
                        nbq = (4 * sb + i) // 2
                        gcols = bk32[T1][:, 512 - 16 * (i + 1): 512 - 16 * i]
                        P.op("dve", I("tensor_copy", out=g16[i][:, 0:nbq], in_=gcols[:, 0:nbq]),
                             reads=[BK[T1]], writes=[f"g16_{i}"])
                return f

            def s_g2(h):
                def f():
                    for i in range(4):
                        nbq = (4 * sb + i) // 2
                        P.op("dve", I("max", out=m8[i], in_=g16[i]), reads=[f"g16_{i}"], writes=[f"m8_{i}"])
                        P.op("dve", I("tensor_scalar", out=thr[i], in0=m8[i][:, 2:3], scalar1=-1e29, scalar2=None,
                                      op0=ALU.max), reads=[f"m8_{i}"], writes=[f"thr{i}"])
                        P.op("dve", I("tensor_scalar", out=negm[i][:, 0:nbq], in0=g16[i][:, 0:nbq], scalar1=thr[i],
                                      scalar2=NEG, op0=ALU.is_lt, op1=ALU.mult),
                             reads=[f"g16_{i}", f"thr{i}"], writes=[f"negm{i}"])
                return f

            def s_g3(h):
                def f():
                    for i in range(4):
                        P.op("pe", I("transpose", out=bk16[T1][:, i * 128:(i + 1) * 128], in_=negm[i], identity=ident),
                             reads=[f"negm{i}", "ident"], writes=[BK[T1]])
                    P.op("act", I("activation", out=negT[hs][h], in_=bk16[T1][:, 0:512], func=AF.Copy),
                         reads=[BK[T1]], writes=[f"negT{hs}_{h}"])
                return f

            steps = []
            nb_ = len(blocks)
            for b in range(nb_):
                if b == 0:
                    steps += [s_pe(0, 0), s_pe(0, 1)]
                else:
                    steps += [s_pe(b, 0), s_mid(b - 1), s_pe(b, 1), s_post(b - 1)]
            steps += [s_mid(nb_ - 1), s_post(nb_ - 1)]
            if (not isA) and sb >= 2:
                for h in range(nh):
                    steps += [s_g1(h), s_g2(h), s_g3(h)]
            return steps

        def att_steps(sb):
            hs = sb % 2
            QT = QTs[hs]
            steps = []

            def zero():
                zero_init(O0)
                zero_init(O1)

            def pv(PT, pname, col0, h, kt, rr):
                ob = (O0, O1)[rr // 2]
                oc = (rr % 2) * 129
                P.op("pe", I("matmul", bk32[ob][:, oc:oc + 129], lhsT=PT[:, col0:col0 + 128],
                             rhs=Vst[:, h * 32 + kt, 0:129], start=False, stop=False, skip_group_check=True),
                     reads=[pname, f"V{h}_{kt // 4}", "vones"], writes=[BK[ob]])

            def make_units_A():
                Ds = [1, 0] if sb >= 1 else [0]
                units = []
                for D in Ds:
                    for rp in range(4):
                        units.append(("g0", D, rp))
                for D in Ds:
                    units.append(("pk", 1, 4096 + (0 if D == 0 else 128), D))
                for m in range(5):
                    if sb - m >= 0:
                        units.append(("pk", 2, 4096 + 256 + 128 * m, m))
                return units

            def qk_A(u, st):
                def f():
                    ab = (A0, A1)[nxt("a", 2)]
                    st["ab"] = ab
                    if u[0] == "g0":
                        _, D, rp = u
                        kt = (sb - D) * 4 + rp
                        P.op("pe", I("matmul", bk32[ab], lhsT=KT[:, 0, kt * 128:(kt + 1) * 128], rhs=QT[:, 0, :],
                                     start=True, stop=False), reads=[f"KT0_{kt // 4}", f"QT{hs}_0"], writes=[BK[ab]])
                        bo = (D * 4 + rp) * 512
                        P.op("pe", I("matmul", bk32[ab], lhsT=ident, rhs=bA[:, bo:bo + 512], start=False, stop=True),
                             reads=["ident", "bA"], writes=[BK[ab]])
                    else:
                        _, h, bo, dsb = u
                        for rr in range(4):
                            kt = (sb - dsb) * 4 + rr
                            P.op("pe", I("matmul", bk32[ab][:, rr * 128:(rr + 1) * 128], lhsT=KT[:, h, kt * 128:(kt + 1) * 128],
                                         rhs=QT[:, h, rr * 128:(rr + 1) * 128], start=True, stop=False),
                                 reads=[f"KT{h}_{kt // 4}", f"QT{hs}_{h}"], writes=[BK[ab]])
                            P.op("pe", I("matmul", bk32[ab][:, rr * 128:(rr + 1) * 128], lhsT=ident, rhs=bA[:, bo:bo + 128],
                                         start=False, stop=True), reads=["ident", "bA"], writes=[BK[ab]])
                return f

            def ep_A(u, st):
                def f():
                    ab = st["ab"]
                    ps = nxt("pt", 3)
                    PT = PTs[ps]
                    P.op("act", ACT(AF.Exp, PT, bk32[ab]), reads=[BK[ab]], writes=[f"PT{ps}"])
                    for rr in range(4):
                        if u[0] == "g0":
                            pv(PT, f"PT{ps}", rr * 128, 0, (sb - u[1]) * 4 + u[2], rr)
                        else:
                            pv(PT, f"PT{ps}", rr * 128, u[1], (sb - u[3]) * 4 + rr, rr)
                return f

            def qk_B(h, j, st):
                def f():
                    u = max(0, j - 4 * sb)
                    q0 = 128 * u
                    N = 512 - q0
                    n = j // 2
                    need_mask = (n <= 2 * sb) and sb >= 2
                    ab = (A0, A1)[nxt("a", 2)]
                    st["ab"] = ab
                    P.op("pe", I("matmul", bk32[ab][:, 0:N], lhsT=KT[:, h, j * 128:(j + 1) * 128], rhs=QT[:, h, q0:512],
                                 start=True, stop=False), reads=[f"KT{h}_{j // 4}", f"QT{hs}_{h}"], writes=[BK[ab]])
                    t0 = 512 * sb + q0 - 128 * j
                    P.op("pe", I("matmul", bk32[ab][:, 0:N], lhsT=ident, rhs=TT[:, h, t0:t0 + N], start=False,
                                 stop=(not need_mask)), reads=["ident", "TT"], writes=[BK[ab]])
                    if need_mask:
                        P.op("pe", I("matmul", bk32[ab][:, 0:N], lhsT=em[:, n, :], rhs=negT[hs][h][:, q0:512],
                                     start=False, stop=True), reads=["em", f"negT{hs}_{h}"], writes=[BK[ab]])
                return f

            def ep_B(h, j, st):
                def f():
                    u = max(0, j - 4 * sb)
                    q0 = 128 * u
                    N = 512 - q0
                    ab = st["ab"]
                    ps = nxt("pt", 3)
                    PT = PTs[ps]
                    P.op("act", I("activation", out=PT[:, 0:N], in_=bk32[ab][:, 0:N], func=AF.Exp),
                         reads=[BK[ab]], writes=[f"PT{ps}"])
                    for i in range(u, 4):
                        pv(PT, f"PT{ps}", i * 128 - q0, h, j, i)
                return f

            def fin(i):
                def f():
                    ob, oc = (O0, O1)[i // 2], (i % 2) * 129
                    rec = small[:, 16 + i: 17 + i]
                    P.op("dve", I("reciprocal", out=rec, in_=bk32[ob][:, oc + 128: oc + 129]), reads=[BK[ob]], writes=[f"rec{i}"])
                    P.op("act", I("activation", out=ybufs[i], in_=bk32[ob][:, oc: oc + 128], func=AF.Copy, scale=rec),
                         reads=[BK[ob], f"rec{i}"], writes=[f"yb{i}"])
                return f

            def store(row0):
                def f():
                    for i in range(4):
                        P.op("pe", I("transpose", out=bk16[T1][:, i * 128:(i + 1) * 128], in_=ybufs[i], identity=ident),
                             reads=[f"yb{i}", "ident"], writes=[BK[T1]])
                    ys = nxt("y", 2)
                    yst = ystage[ys]
                    if isA:
                        P.op("dve", I("tensor_copy", out=yst.rearrange("p (a r) -> p r a", r=4),
                                      in_=bk16[T1][:, 0:512].rearrange("p (r a) -> p r a", r=4)),
                             reads=[BK[T1]], writes=[f"yst{ys}"])
                    else:
                        P.op("dve", I("tensor_copy", out=yst, in_=bk16[T1][:, 0:512]), reads=[BK[T1]], writes=[f"yst{ys}"])
                    cch = sb // 2
                    P.op("sp", I("dma_start", out=bounce.ap()[cch, row0:row0 + 128, (sb % 2) * 512:(sb % 2) * 512 + 512],
                                 in_=yst), reads=[f"yst{ys}"], writes=[f"bnc{cch}_{sb % 2}_{row0}"], dma_key=f"yst{ys}")
                return f

            def pipeline(qks, eps):
                out_ = [zero]
                n_ = len(qks)
                out_.append(qks[0])
                if n_ > 1:
                    out_.append(qks[1])
                for i in range(n_):
                    out_.append(eps[i])
                    if i + 2 < n_:
                        out_.append(qks[i + 2])
                return out_

            if isA:
                units = make_units_A()
                sts = [dict() for _ in units]
                steps += pipeline([qk_A(u, s_) for u, s_ in zip(units, sts)], [ep_A(u, s_) for u, s_ in zip(units, sts)])
                steps += [fin(i) for i in range(4)] + [store(0)]
            else:
                for h in range(2):
                    js = list(range(4 * sb + 4))
                    sts = [dict() for _ in js]
                    steps += pipeline([qk_B(h, j, s_) for j, s_ in zip(js, sts)], [ep_B(h, j, s_) for j, s_ in zip(js, sts)])
                    steps += [fin(i) for i in range(4)] + [store(128 * (1 + h))]
                if sb % 2 == 1:
                    def cc():
                        cch = sb // 2
                        rd = [f"bnc{cch}_{s_}_{r_}" for s_ in range(2) for r_ in (128, 256)]
                        cc_evs.append(P.op("pool", I("collective_compute", "AllGather", ALU.bypass, replica_groups=GROUPS,
                                                     ins=[bounce.ap()[cch]], outs=[gath.ap()[cch * 1536:(cch + 1) * 1536, :]]),
                                           reads=rd, writes=[f"gath{cch}"], dma_key=f"cc{cch}", inc=1))
                    steps.append(cc)
            return steps

        for it in range(10):
            lists = []
            if it < 8:
                lists.append(fe_steps(it))
            if 0 <= it - 1 < 8:
                lists.append(proj_steps(it - 1))
            if 0 <= it - 2 < 8:
                lists.append(att_steps(it - 2))
            merge(lists)
        P.barrier()

    run_pass("A")
    run_pass("B")

    AR.reset(base_mark)
    mixed = AR.alloc(BF16, 16, 1024)
    p2_mark = AR.mark()
    fe_alloc()
    hTo = AR.alloc(BF16, 16, 1024)
    AT = AR.alloc(BF16, 12, 1024)
    wgs = [[AR.alloc(BF16, 16, 128) for _ in range(2)] for _ in range(2)]
    wbas = [AR.alloc(BF16, 4, 128) for _ in range(2)]
    wbbs = [AR.alloc(BF16, 8, 128) for _ in range(2)]
    sig = [[AR.alloc(F32, 512) for _ in range(2)] for _ in range(2)]
    tmp = [[AR.alloc(F32, 512) for _ in range(2)] for _ in range(2)]

    for c in range(12):
        P.op("pool", I("indirect_dma_start", out=AT[:, c, :], out_offset=None, in_=gath.ap(),
                       in_offset=bass.IndirectOffsetOnAxis(ap=idx[:, c:c + 1], axis=0)),
             reads=["idx"], writes=[f"AT{c}"], dma_key=f"at{c % 4}", extra=cc_evs)

    def load_fc(fc):
        s = fc % 2
        P.op("pool", I("dma_start", out=wgs[0][s], in_=wga.ap()[fc]), writes=[f"wga{s}"], dma_key=f"wga{s}")
        P.op("pool", I("dma_start", out=wgs[1][s], in_=wgb.ap()[fc]), writes=[f"wgb{s}"], dma_key=f"wgb{s}")
        P.op("pool", I("dma_start", out=wbas[s], in_=wba.ap()[fc]), writes=[f"wba{s}"], dma_key=f"wba{s}")
        P.op("pool", I("dma_start", out=wbbs[s], in_=wbb.ap()[fc]), writes=[f"wbb{s}"], dma_key=f"wbb{s}")

    load_fc(0)
    xo = xown.ap()
    for i in range(8):
        frontend_tile(xo[i * 128:(i + 1) * 128, :],
                      (lambda i=i: lambda half: hTo[:, half * 8:(half + 1) * 8, i * 128:(i + 1) * 128])(),
                      [f"hTo_{i}_0", f"hTo_{i}_1"], (0, 1))
    hTo_names = [[f"hTo_{i}_{h}" for i in range(4 * th, 4 * th + 4) for h in range(2)] for th in range(2)]
    at_chunk_a = [0, 3, 6, 9]
    at_chunk_b = [1, 2, 4, 5, 7, 8, 10, 11]
    it = 0
    for fc in range(16):
        s = fc % 2
        if fc + 1 < 16:
            load_fc(fc + 1)
        for th in range(2):
            k = it % 2
            it += 1
            GA, GB, YA, YB = 4 * k, 4 * k + 1, 4 * k + 2, 4 * k + 3
            tsl = slice(th * 512, th * 512 + 512)
            for kind, bi in ((0, GA), (1, GB)):
                wt = wgs[kind][s]
                wname = ("wga%d" if kind == 0 else "wgb%d") % s
                for c in range(16):
                    P.op("pe", I("matmul", bk32[bi], lhsT=wt[:, c, :], rhs=hTo[:, c, tsl], start=(c == 0), stop=(c == 15)),
                         reads=hTo_names[th] + [wname], writes=[BK[bi]])
                P.op("act", ACT(AF.Sigmoid, sig[kind][k], bk32[bi]), reads=[BK[bi]], writes=[f"sig{kind}_{k}"])
            for q in range(4):
                P.op("pe", I("matmul", bk32[YA], lhsT=wbas[s][:, q, :], rhs=AT[:, at_chunk_a[q], tsl],
                             start=(q == 0), stop=(q == 3)), reads=[f"wba{s}", f"AT{at_chunk_a[q]}"], writes=[BK[YA]])
            for q in range(8):
                P.op("pe", I("matmul", bk32[YB], lhsT=wbbs[s][:, q, :], rhs=AT[:, at_chunk_b[q], tsl],
                             start=(q == 0), stop=(q == 7)), reads=[f"wbb{s}", f"AT{at_chunk_b[q]}"], writes=[BK[YB]])
            P.op("dve", I("tensor_tensor", out=tmp[0][k], in0=bk32[YA], in1=sig[0][k], op=ALU.mult),
                 reads=[BK[YA], f"sig0_{k}"], writes=[f"tmp0_{k}"])
            P.op("dve", I("tensor_tensor", out=tmp[1][k], in0=bk32[YB], in1=sig[1][k], op=ALU.mult),
                 reads=[BK[YB], f"sig1_{k}"], writes=[f"tmp1_{k}"])
            P.op("dve", I("tensor_tensor", out=mixed[:, fc, tsl], in0=tmp[0][k], in1=tmp[1][k], op=ALU.add),
                 reads=[f"tmp0_{k}", f"tmp1_{k}"], writes=[f"mixed{fc}_{th}"])
    if debug:
        P.op("sp", I("dma_start", out=dbg_at.ap().rearrange("(c p) t -> p c t", p=128), in_=AT),
             reads=[f"AT{c}" for c in range(12)], writes=["dbg_at"], dma_key="dbg0")

    P.barrier()

    AR.reset(p2_mark)
    resid = AR.alloc(F32, 8, D_MODEL)
    res_mark = AR.mark()
    wos = [AR.alloc(BF16, 16, 512) for _ in range(2)]
    RES = [[f"res{i}_{nb}" for nb in range(4)] for i in range(8)]

    for i in range(8):
        P.op("sp", I("dma_start", out=resid[:, i, :], in_=xo[i * 128:(i + 1) * 128, :]), writes=RES[i], dma_key=f"res{i % 4}")

    def load_wo(nb):
        s = nb % 2
        for h2 in range(2):
            P.op("pool", I("dma_start", out=wos[s][:, 8 * h2:8 * h2 + 8, :], in_=wo.ap()[nb][:, 8 * h2:8 * h2 + 8, :]),
                 writes=[f"wo{s}"], dma_key=f"wo{s}_{h2}")

    load_wo(0)
    it = 0
    for nb in range(4):
        s = nb % 2
        if nb + 1 < 4:
            load_wo(nb + 1)
        for i in range(8):
            bi = it % 4
            it += 1
            for c in range(16):
                P.op("pe", I("matmul", bk32[bi], lhsT=mixed[:, c, i * 128:(i + 1) * 128], rhs=wos[s][:, c, :],
                             start=(c == 0), stop=(c == 15)), reads=[f"wo{s}"], writes=[BK[bi]])
            rsl = resid[:, i, nb * 512:(nb + 1) * 512]
            P.op("dve", I("tensor_tensor", out=rsl, in0=bk32[bi], in1=rsl, op=ALU.add),
                 reads=[BK[bi], RES[i][nb]], writes=[RES[i][nb]])
    if debug:
        P.op("sp", I("dma_start", out=dbg_x1.ap().rearrange("(i p) n -> p i n", p=128), in_=resid),
             reads=[r for rr_ in RES for r in rr_], writes=["dbg_x1"], dma_key="dbg1")

    P.op("sp", I("dma_start", out=gb, in_=gmlpb.ap()), reads=[], writes=["gb"], dma_key="c2")
    P.barrier()
    AR.reset(base_mark)
    wds = [AR.alloc(BF16, 4, 2048) for _ in range(2)]
    assert AR.mark() == p2_mark
    AR.reset(res_mark)
    fe_alloc(xs=False)
    h2T = AR.alloc(BF16, 16, 1024)
    wus = [AR.alloc(BF16, 16, 512) for _ in range(2)]
    uTs = [AR.alloc(BF16, 4, 1024) for _ in range(2)]
    rl = [AR.alloc(F32, 512) for _ in range(2)]

    def load_hg(hg):
        s = hg % 2
        for h2 in range(2):
            P.op("pool", I("dma_start", out=wus[s][:, 8 * h2:8 * h2 + 8, :], in_=wu.ap()[hg][:, 8 * h2:8 * h2 + 8, :]),
                 writes=[f"wu{s}"], dma_key=f"wu{s}_{h2}")
        for h2 in range(2):
            P.op("pool", I("dma_start", out=wds[s][:, 2 * h2:2 * h2 + 2, :], in_=wd.ap()[hg][:, 2 * h2:2 * h2 + 2, :]),
                 writes=[f"wd{s}"], dma_key=f"wd{s}_{h2}")

    load_hg(0)
    for i in range(8):
        frontend_tile(resid[:, i, :],
                      (lambda i=i: lambda half: h2T[:, half * 8:(half + 1) * 8, i * 128:(i + 1) * 128])(),
                      [f"h2T_{i}_0", f"h2T_{i}_1"], (0, 1), xs_res=RES[i])
    h2T_names = [[f"h2T_{i}_{h}" for i in range(4 * th, 4 * th + 4) for h in range(2)] for th in range(2)]
    ucnt = 0
    dcnt = 0
    for hg in range(16):
        s = hg % 2
        if hg + 1 < 16:
            load_hg(hg + 1)
        uT = uTs[s]
        for th in range(2):
            tsl = slice(th * 512, th * 512 + 512)
            for hc in range(4):
                bi = ucnt % 3
                k = ucnt % 2
                ucnt += 1
                for c in range(16):
                    P.op("pe", I("matmul", bk32[bi], lhsT=wus[s][:, c, hc * 128:(hc + 1) * 128], rhs=h2T[:, c, tsl],
                                 start=(c == 0), stop=(c == 15)), reads=h2T_names[th] + [f"wu{s}"], writes=[BK[bi]])
                P.op("act", ACT(AF.Relu, rl[k], bk32[bi]), reads=[BK[bi]], writes=[f"rl{k}"])
                P.op("dve", I("tensor_tensor", out=uT[:, hc, tsl], in0=rl[k], in1=rl[k], op=ALU.mult),
                     reads=[f"rl{k}"], writes=[f"uT{s}_{hc}_{th}"])
        for i in range(8):
            th = i // 4
            for nb in range(4):
                bi = 3 + dcnt % 5
                dcnt += 1
                for hc in range(4):
                    P.op("pe", I("matmul", bk32[bi], lhsT=uT[:, hc, i * 128:(i + 1) * 128],
                                 rhs=wds[s][:, hc, nb * 512:(nb + 1) * 512], start=(hc == 0), stop=(hc == 3)),
                         reads=[f"uT{s}_{hc}_{th}", f"wd{s}"], writes=[BK[bi]])
                rsl = resid[:, i, nb * 512:(nb + 1) * 512]
                P.op("dve", I("tensor_tensor", out=rsl, in0=bk32[bi], in1=rsl, op=ALU.add),
                     reads=[BK[bi], RES[i][nb]], writes=[RES[i][nb]])
    outs = []
    for i in range(8):
        outs.append(P.op("sp", I("dma_start", out=out.ap()[i * 128:(i + 1) * 128, :], in_=resid[:, i, :]),
                         reads=RES[i], writes=[f"out{i}"], dma_key=f"out{i % 4}"))
    P.op("sp", None, extra=outs)
    P.barrier()

    with nc.Block() as block:
        P.emit(block)
    return nc


def _rel_bucket(dist):
    d = np.maximum(dist, 0)
    df = np.maximum(d, 1).astype(np.float32)
    large = 16 + (np.log(df / np.float32(16)) / np.float32(np.log(2048 / 16)) * np.float32(16)).astype(np.int32)
    large = np.minimum(large, 31)
    return np.where(d < 16, d, large)


def _bias_tables(rel_bias, heads_a, heads_b):
    a = np.arange(128)
    ap_, aq = a[:, None], a[None, :]
    negf = np.float32(NEG)

    def pat(head, dist, valid):
        v = rel_bias[_rel_bucket(dist), head].astype(np.float32)
        return np.where(valid, v, negf).astype(np.float32)

    blocks = []
    for D in range(2):
        for rp in range(4):
            cols = []
            for rr in range(4):
                dist = 512 * D + 4 * (aq - ap_) + (rr - rp)
                cols.append(pat(heads_a[0], dist, (dist >= 0) & (dist <= 128)))
            blocks.append(np.concatenate(cols, axis=1))
    for D in range(2):
        dist = 512 * D + 4 * (aq - ap_)
        blocks.append(pat(heads_a[1], dist, (dist >= 0) & (dist <= 512)))
    for m in range(5):
        dist = 512 * m + 4 * (aq - ap_)
        blocks.append(pat(heads_a[2], dist, (dist >= 0) & (dist % 16 == 0) & (dist <= 2048)))
    biasA = np.ascontiguousarray(np.concatenate(blocks, axis=1))
    assert biasA.shape == (128, 4992)
    cc = np.arange(4096)[None, :]
    tt = []
    for hb in heads_b:
        dist = cc - ap_
        tt.append(pat(hb, dist, dist >= 0))
    ttB = np.ascontiguousarray(np.concatenate(tt, axis=1))
    return biasA, ttB


def _prep_inputs(x, g_mix, w_in, q_norm_a, k_norm_a, q_norm_b, k_norm_b, rel_bias,
                 w_branch_a, w_branch_b, w_out, g_mlp, w_up, w_down):
    f = lambda a: np.ascontiguousarray(np.asarray(a, dtype=np.float32))
    x, w_in, rel_bias = f(x), f(w_in), f(rel_bias)
    shared = {}
    shared["gmixb"] = np.ascontiguousarray(np.broadcast_to(f(g_mix)[None, :], (128, D_MODEL)))
    shared["gmlpb"] = np.ascontiguousarray(np.broadcast_to(f(g_mlp)[None, :], (128, D_MODEL)))
    shared["qkg"] = np.ascontiguousarray(np.stack([f(q_norm_a), f(k_norm_a), f(q_norm_b), f(k_norm_b)], axis=1))
    shared["identd"] = np.eye(128, dtype=np.float32)
    em = np.zeros((128, 16, 128), np.float32)
    for n in range(16):
        em[n, n, :] = 1.0
    shared["ematd"] = em.reshape(128, 16 * 128)
    wg = w_in[:, A_QKV + B_QKV:]
    lay = lambda w, nf: np.ascontiguousarray(w.reshape(16, 128, nf, 128).transpose(2, 1, 0, 3))
    shared["wga"] = lay(wg[:, :D_MODEL], 16)
    shared["wgb"] = lay(wg[:, D_MODEL:], 16)
    shared["wba"] = np.ascontiguousarray(f(w_branch_a).reshape(4, 128, 16, 128).transpose(2, 1, 0, 3))
    shared["wbb"] = np.ascontiguousarray(f(w_branch_b).reshape(8, 128, 16, 128).transpose(2, 1, 0, 3))
    shared["wo"] = np.ascontiguousarray(f(w_out).reshape(16, 128, 4, 512).transpose(2, 1, 0, 3))
    shared["wu"] = np.ascontiguousarray(f(w_up).reshape(16, 128, 16, 512).transpose(2, 1, 0, 3))
    shared["wd"] = np.ascontiguousarray(f(w_down).reshape(16, 4, 128, 2048).transpose(0, 2, 1, 3))
    in_maps = []
    for c in range(8):
        b, j = divmod(c, 4)
        m = dict(shared)
        m["xfull"] = np.ascontiguousarray(x[b])
        m["xown"] = np.ascontiguousarray(x[b, 1024 * j:1024 * (j + 1)])
        heads_a = [j, 4 + j, 8 + j]
        cols = []
        for h in heads_a:
            for t in range(3):
                cols.append(w_in[:, t * 1536 + h * 128: t * 1536 + (h + 1) * 128])
        wa = np.concatenate(cols, axis=1)
        m["wA"] = np.ascontiguousarray(wa.reshape(16, 128, 1152).transpose(1, 0, 2))
        heads_b = [2 * j, 2 * j + 1]
        cols = []
        for h in heads_b:
            for t in range(3):
                cols.append(w_in[:, A_QKV + t * 1024 + h * 128: A_QKV + t * 1024 + (h + 1) * 128])
        wb = np.concatenate(cols, axis=1)
        m["wB"] = np.ascontiguousarray(wb.reshape(16, 128, 768).transpose(1, 0, 2))
        m["biasA"], m["ttB"] = _bias_tables(rel_bias, heads_a, [A_HEADS + h for h in heads_b])
        m["idxd"] = (j * 1536 + np.arange(12)[None, :] * 128 + np.arange(128)[:, None]).astype(np.uint32)
        in_maps.append(m)
    return in_maps


_CACHE = {}


def kernel(**inputs):
    debug = bool(os.environ.get("MK_DEBUG"))
    key = ("nc", debug)
    if key not in _CACHE:
        _CACHE[key] = build_program(debug)
    nc = _CACHE[key]
    in_maps = _prep_inputs(**inputs)
    res = run_bass_kernel_spmd(nc, in_maps, core_ids=list(range(8)))
    outs = [np.asarray(r["out"], dtype=np.float32) for r in res.results]
    full = np.stack([np.concatenate(outs[0:4], axis=0), np.concatenate(outs[4:8], axis=0)], axis=0)
    if debug:
        kernel.debug = res.results
    return full
```
